# Optimizing a Trainium2 kernel written in Bass

```python
import math
import jax
import jax.numpy as jnp
from jax import lax
import numpy as np

D_MODEL = 1024
BATCH = 8
SEQ = 4096
DEPTH = 2

D_FF = 2816
MLA_HEADS = 4
MLA_NOPE_DIM = 64
MLA_ROPE_DIM = 32
MLA_V_DIM = 64
MLA_Q_LORA = 256
MLA_KV_LORA = 128
DIFF_HEADS = 4
DIFF_QK_DIM = 32
DIFF_V_DIM = 64
NSA_HEADS = 8
NSA_GROUPS = 2
NSA_DK = 64
NSA_DV = 64
CMP_BLOCK = 32
CMP_STRIDE = 16
CMP_HIDDEN = 256
SEL_BLOCK = 64
SEL_TOPK = 16
WINDOW = 512
NSA_Q_BLOCK = 64
ATTN_Q_BLOCK = 128
ROPE_THETA = 10000.0
EPS = 1e-6
NEG = -1e30
FORCE = 1e4

D_MIX = MLA_HEADS * MLA_V_DIM + DIFF_HEADS * DIFF_V_DIM + NSA_HEADS * NSA_DV
IN_SPLITS = (
    MLA_Q_LORA,
    MLA_KV_LORA,
    MLA_ROPE_DIM,
    DIFF_HEADS * 2 * DIFF_QK_DIM,
    DIFF_HEADS * 2 * DIFF_QK_DIM,
    DIFF_HEADS * DIFF_V_DIM,
    NSA_HEADS * NSA_DK,
    3 * 2 * NSA_GROUPS * NSA_DK,
    NSA_HEADS * 3,
)
N_IN = sum(IN_SPLITS)

kernel_name = 'hymba_mla_diff_nsa_macaron'


def rms_norm(x, g):
    xf = x.astype(jnp.float32)
    y = xf * lax.rsqrt(jnp.mean(xf * xf, axis=-1, keepdims=True) + EPS)
    return (y * g.astype(jnp.float32)).astype(x.dtype)


def rope_tables(seq, dim):
    inv = ROPE_THETA ** (-jnp.arange(0, dim, 2, dtype=jnp.float32) / dim)
    ang = jnp.arange(seq, dtype=jnp.float32)[:, None] * inv[None, :]
    return jnp.cos(ang), jnp.sin(ang)


def apply_rope(x, cos, sin):
    half = x.shape[-1] // 2
    x1, x2 = x[..., :half], x[..., half:]
    c = cos[None, :, None, :].astype(x.dtype)
    s = sin[None, :, None, :].astype(x.dtype)
    return jnp.concatenate([x1 * c - x2 * s, x2 * c + x1 * s], axis=-1)


def swiglu(h, wg, wu, wd):
    return (jax.nn.silu(h @ wg) * (h @ wu)) @ wd


def to_blocks(a, nb, qb):
    return a.reshape((a.shape[0], nb, qb) + a.shape[2:]).swapaxes(0, 1)


def from_blocks(a):
    nb, b, qb = a.shape[:3]
    return a.swapaxes(0, 1).reshape((b, nb * qb) + a.shape[3:])


def mla_attention(q_nope, q_rope, k_nope, k_rope, v):
    t = q_nope.shape[1]
    nb = t // ATTN_Q_BLOCK
    kpos = jnp.arange(t)
    scale = (MLA_NOPE_DIM + MLA_ROPE_DIM) ** -0.5

    def block(args):
        qn, qr, bi = args
        qpos = bi * ATTN_Q_BLOCK + jnp.arange(ATTN_Q_BLOCK)
        s = (jnp.einsum('bqhd,bkhd->bhqk', qn, k_nope)
             + jnp.einsum('bqhr,bkr->bhqk', qr, k_rope)).astype(jnp.float32) * scale
        s = jnp.where(kpos[None, :] <= qpos[:, None], s, NEG)
        p = jax.nn.softmax(s, axis=-1).astype(v.dtype)
        return jnp.einsum('bhqk,bkhd->bqhd', p, v)

    out = lax.map(block, (to_blocks(q_nope, nb, ATTN_Q_BLOCK),
                          to_blocks(q_rope, nb, ATTN_Q_BLOCK), jnp.arange(nb)))
    return from_blocks(out)


def diff_attention(q, k, v, lam):
    t = q.shape[1]
    nb = t // ATTN_Q_BLOCK
    kpos = jnp.arange(t)
    scale = DIFF_QK_DIM ** -0.5

    def block(args):
        qb, bi = args
        qpos = bi * ATTN_Q_BLOCK + jnp.arange(ATTN_Q_BLOCK)
        s = jnp.einsum('bqhmd,bkhmd->bhmqk', qb, k).astype(jnp.float32) * scale
        s = jnp.where(kpos[None, :] <= qpos[:, None], s, NEG)
        p = jax.nn.softmax(s, axis=-1)
        a = p[:, :, 0] - lam * p[:, :, 1]
        return jnp.einsum('bhqk,bkhd->bqhd', a.astype(v.dtype), v)

    out = lax.map(block, (to_blocks(q, nb, ATTN_Q_BLOCK), jnp.arange(nb)))
    return from_blocks(out)


def compress_tokens(tok, pos_emb, w1, b1, w2):
    b, t, g, d = tok.shape
    nc = (t - CMP_BLOCK) // CMP_STRIDE + 1
    idx = np.arange(nc)[:, None] * CMP_STRIDE + np.arange(CMP_BLOCK)[None, :]
    blk = tok[:, idx] + pos_emb[None, None, :, None, :]
    flat = blk.transpose(0, 1, 3, 2, 4).reshape(b, nc, g, CMP_BLOCK * d)
    return jax.nn.silu(flat @ w1 + b1) @ w2


def nsa_attention(q, k_cmp, v_cmp, k_slc, v_slc, k_win, v_win, gates):
    b, t, h, dk = q.shape
    g = NSA_GROUPS
    hg = h // g
    dv = v_slc.shape[-1]
    nc = k_cmp.shape[1]
    nb_sel = t // SEL_BLOCK
    n_sel = min(SEL_TOPK, nb_sel)
    qn = NSA_Q_BLOCK
    nq = t // qn
    scale = dk ** -0.5
    cmp_end = jnp.arange(nc) * CMP_STRIDE + CMP_BLOCK - 1
    r_sel = SEL_BLOCK // CMP_STRIDE
    r_cmp = CMP_BLOCK // CMP_STRIDE
    map_idx = (np.arange(nb_sel)[:, None, None] * r_sel
               - np.arange(r_sel)[None, :, None]
               - np.arange(r_cmp)[None, None, :]).reshape(nb_sel, -1)
    map_valid = (map_idx >= 0) & (map_idx < nc)
    map_idx = np.clip(map_idx, 0, nc - 1)
    k_blk = k_slc.reshape(b, nb_sel, SEL_BLOCK, g, dk).transpose(0, 3, 1, 2, 4)
    v_blk = v_slc.reshape(b, nb_sel, SEL_BLOCK, g, dv).transpose(0, 3, 1, 2, 4)
    k_wp = jnp.pad(k_win, ((0, 0), (WINDOW, 0), (0, 0), (0, 0)))
    v_wp = jnp.pad(v_win, ((0, 0), (WINDOW, 0), (0, 0), (0, 0)))
    bi = jnp.arange(b)[:, None, None, None]
    gi = jnp.arange(g)[None, :, None, None]
    blk_ids = jnp.arange(nb_sel)
    tok_in_blk = jnp.arange(SEL_BLOCK)
    win_off = jnp.arange(qn + WINDOW)

    def chunk(args):
        qc, gc, ci = args
        qg = qc.reshape(b, qn, g, hg, dk)
        tpos = ci * qn + jnp.arange(qn)
        s = jnp.einsum('bqghd,bngd->bghqn', qg, k_cmp).astype(jnp.float32) * scale
        valid = cmp_end[None, :] <= tpos[:, None]
        p_c = jax.nn.softmax(jnp.where(valid, s, NEG), axis=-1) * valid
        o_cmp = jnp.einsum('bghqn,bngd->bqghd', p_c.astype(v_cmp.dtype), v_cmp)
        pg = p_c.sum(axis=2)
        p_slc = jnp.where(map_valid, pg[..., map_idx], 0.0).sum(-1)
        cblk = (tpos // SEL_BLOCK)[:, None]
        forced = (blk_ids == 0) | (blk_ids == cblk) | (blk_ids == cblk - 1)
        score = jnp.where(blk_ids > cblk, -1.0, p_slc + jnp.where(forced, FORCE, 0.0))
        _, sel = lax.top_k(score, n_sel)
        k_sel = k_blk[bi, gi, sel]
        v_sel = v_blk[bi, gi, sel]
        kpos = sel[..., None] * SEL_BLOCK + tok_in_blk
        m_s = (kpos <= tpos[None, None, :, None, None])[:, :, None]
        s = jnp.einsum('bqghd,bgqnkd->bghqnk', qg, k_sel).astype(jnp.float32) * scale
        p_s = jax.nn.softmax(jnp.where(m_s, s, NEG), axis=(-2, -1))
        o_slc = jnp.einsum('bghqnk,bgqnkd->bqghd', p_s.astype(v_sel.dtype), v_sel)
        k_w = lax.dynamic_slice_in_dim(k_wp, ci * qn, qn + WINDOW, axis=1)
        v_w = lax.dynamic_slice_in_dim(v_wp, ci * qn, qn + WINDOW, axis=1)
        kpos_w = ci * qn - WINDOW + win_off
        delta = tpos[:, None] - kpos_w[None, :]
        m_w = (delta >= 0) & (delta < WINDOW) & (kpos_w >= 0)[None, :]
        s = jnp.einsum('bqghd,bkgd->bghqk', qg, k_w).astype(jnp.float32) * scale
        p_w = jax.nn.softmax(jnp.where(m_w, s, NEG), axis=-1)
        o_win = jnp.einsum('bghqk,bkgd->bqghd', p_w.astype(v_w.dtype), v_w)
        gt = gc.reshape(b, qn, g, hg, 3)
        o = gt[..., 0:1] * o_cmp + gt[..., 1:2] * o_slc + gt[..., 2:3] * o_win
        return o.reshape(b, qn, h, dv)

    out = lax.map(chunk, (to_blocks(q, nq, qn), to_blocks(gates, nq, qn), jnp.arange(nq)))
    return from_blocks(out)


def token_mixer(h, w_in, mla_q_norm_g, mla_w_uq, mla_kv_norm_g, mla_w_ukv,
                diff_lambda, diff_norm_g, lam_init,
                cmp_pos, cmp_w1, cmp_b1, cmp_w2, gate_b, w_out,
                cs_mla, cs_diff, cs_nsa):
    b, t, _ = h.shape
    splits = np.cumsum(IN_SPLITS)[:-1].tolist()
    cq, ckv, kr, dq, dkk, dvv, nq, nkv, ng = jnp.split(h @ w_in, splits, axis=-1)

    qm = (rms_norm(cq, mla_q_norm_g) @ mla_w_uq).reshape(b, t, MLA_HEADS, MLA_NOPE_DIM + MLA_ROPE_DIM)
    q_nope = qm[..., :MLA_NOPE_DIM]
    q_rope = apply_rope(qm[..., MLA_NOPE_DIM:], *cs_mla)
    kvm = (rms_norm(ckv, mla_kv_norm_g) @ mla_w_ukv).reshape(b, t, MLA_HEADS, MLA_NOPE_DIM + MLA_V_DIM)
    k_nope = kvm[..., :MLA_NOPE_DIM]
    v_mla = kvm[..., MLA_NOPE_DIM:]
    k_rope = apply_rope(kr[:, :, None, :], *cs_mla)[:, :, 0, :]
    o_mla = mla_attention(q_nope, q_rope, k_nope, k_rope, v_mla)

    qd = apply_rope(dq.reshape(b, t, DIFF_HEADS * 2, DIFF_QK_DIM), *cs_diff).reshape(b, t, DIFF_HEADS, 2, DIFF_QK_DIM)
    kd = apply_rope(dkk.reshape(b, t, DIFF_HEADS * 2, DIFF_QK_DIM), *cs_diff).reshape(b, t, DIFF_HEADS, 2, DIFF_QK_DIM)
    vd = dvv.reshape(b, t, DIFF_HEADS, DIFF_V_DIM)
    lf = diff_lambda.astype(jnp.float32)
    lam = jnp.exp(jnp.sum(lf[0] * lf[1])) - jnp.exp(jnp.sum(lf[2] * lf[3])) + lam_init
    o_diff = rms_norm(diff_attention(qd, kd, vd, lam), diff_norm_g) * (1.0 - lam_init)

    qn_ = apply_rope(nq.reshape(b, t, NSA_HEADS, NSA_DK), *cs_nsa)
    kv = nkv.reshape(b, t, 3, 2, NSA_GROUPS, NSA_DK)
    k_all = apply_rope(kv[:, :, :, 0].reshape(b, t, 3 * NSA_GROUPS, NSA_DK), *cs_nsa).reshape(b, t, 3, NSA_GROUPS, NSA_DK)
    v_all = kv[:, :, :, 1]
    k_cmp = compress_tokens(k_all[:, :, 0], cmp_pos[0], cmp_w1[0], cmp_b1[0], cmp_w2[0])
    v_cmp = compress_tokens(v_all[:, :, 0], cmp_pos[1], cmp_w1[1], cmp_b1[1], cmp_w2[1])
    gates = jax.nn.sigmoid(ng + gate_b).reshape(b, t, NSA_HEADS, 3)
    o_nsa = nsa_attention(qn_, k_cmp, v_cmp, k_all[:, :, 1], v_all[:, :, 1],
                          k_all[:, :, 2], v_all[:, :, 2], gates)

    o = jnp.concatenate([o_mla.reshape(b, t, -1), o_diff.reshape(b, t, -1),
                         o_nsa.reshape(b, t, -1)], axis=-1)
    return o @ w_out


def setup_inputs(seed: int = 0) -> dict:
    key = jax.random.key(seed)
    ks = jax.random.split(key, 24)

    def nrm(k, shape, scale):
        return jax.random.normal(k, shape, jnp.float32) * scale

    def gain(k, shape):
        return 1.0 + 0.02 * jax.random.normal(k, shape, jnp.float32)

    return {
        'x': nrm(ks[0], (BATCH, SEQ, D_MODEL), 1.0),
        'ffn_norm_g': gain(ks[1], (DEPTH, 2, D_MODEL)),
        'ffn_w_gate': nrm(ks[2], (DEPTH, 2, D_MODEL, D_FF), D_MODEL ** -0.5),
        'ffn_w_up': nrm(ks[3], (DEPTH, 2, D_MODEL, D_FF), D_MODEL ** -0.5),
        'ffn_w_down': nrm(ks[4], (DEPTH, 2, D_FF, D_MODEL), D_FF ** -0.5),
        'mix_norm_g': gain(ks[5], (DEPTH, D_MODEL)),
        'w_in': nrm(ks[6], (DEPTH, D_MODEL, N_IN), D_MODEL ** -0.5),
        'mla_q_norm_g': gain(ks[7], (DEPTH, MLA_Q_LORA)),
        'mla_w_uq': nrm(ks[8], (DEPTH, MLA_Q_LORA, MLA_HEADS * (MLA_NOPE_DIM + MLA_ROPE_DIM)), MLA_Q_LORA ** -0.5),
        'mla_kv_norm_g': gain(ks[9], (DEPTH, MLA_KV_LORA)),
        'mla_w_ukv': nrm(ks[10], (DEPTH, MLA_KV_LORA, MLA_HEADS * (MLA_NOPE_DIM + MLA_V_DIM)), MLA_KV_LORA ** -0.5),
        'diff_lambda': nrm(ks[11], (DEPTH, 4, DIFF_QK_DIM), 0.1),
        'diff_norm_g': gain(ks[12], (DEPTH, DIFF_V_DIM)),
        'nsa_cmp_pos': nrm(ks[13], (DEPTH, 2, CMP_BLOCK, NSA_DK), 0.02),
        'nsa_cmp_w1': nrm(ks[14], (DEPTH, 2, CMP_BLOCK * NSA_DK, CMP_HIDDEN), (CMP_BLOCK * NSA_DK) ** -0.5),
        'nsa_cmp_b1': nrm(ks[15], (DEPTH, 2, CMP_HIDDEN), 0.01),
        'nsa_cmp_w2': nrm(ks[16], (DEPTH, 2, CMP_HIDDEN, NSA_DK), CMP_HIDDEN ** -0.5),
        'nsa_gate_b': nrm(ks[17], (DEPTH, NSA_HEADS * 3), 0.01),
        'w_out': nrm(ks[18], (DEPTH, D_MIX, D_MODEL), D_MIX ** -0.5),
        'final_norm_g': gain(ks[19], (D_MODEL,)),
    }


def reference(x, ffn_norm_g, ffn_w_gate, ffn_w_up, ffn_w_down, mix_norm_g, w_in,
              mla_q_norm_g, mla_w_uq, mla_kv_norm_g, mla_w_ukv, diff_lambda, diff_norm_g,
              nsa_cmp_pos, nsa_cmp_w1, nsa_cmp_b1, nsa_cmp_w2, nsa_gate_b, w_out,
              final_norm_g):
    t = x.shape[1]
    cs_mla = rope_tables(t, MLA_ROPE_DIM)
    cs_diff = rope_tables(t, DIFF_QK_DIM)
    cs_nsa = rope_tables(t, NSA_DK)
    for l in range(DEPTH):
        lam_init = 0.8 - 0.6 * math.exp(-0.3 * l)
        x = x + 0.5 * swiglu(rms_norm(x, ffn_norm_g[l, 0]), ffn_w_gate[l, 0],
                             ffn_w_up[l, 0], ffn_w_down[l, 0])
        x = x + token_mixer(rms_norm(x, mix_norm_g[l]), w_in[l],
                            mla_q_norm_g[l], mla_w_uq[l], mla_kv_norm_g[l], mla_w_ukv[l],
                            diff_lambda[l], diff_norm_g[l], lam_init,
                            nsa_cmp_pos[l], nsa_cmp_w1[l], nsa_cmp_b1[l], nsa_cmp_w2[l],
                            nsa_gate_b[l], w_out[l], cs_mla, cs_diff, cs_nsa)
        x = x + 0.5 * swiglu(rms_norm(x, ffn_norm_g[l, 1]), ffn_w_gate[l, 1],
                             ffn_w_up[l, 1], ffn_w_down[l, 1])
    return rms_norm(x, final_norm_g)
```

```python
import numpy as np
import concourse.bass as bass
import concourse.mybir as mybir

F32 = mybir.dt.float32
BF16 = mybir.dt.bfloat16
AF = mybir.ActivationFunctionType
ALU = mybir.AluOpType
AX = mybir.AxisListType

NDMASEM = 8


class Tok:
    __slots__ = ("w", "r", "name", "wpar")

    def __init__(self, name=""):
        self.w = []
        self.r = []
        self.name = name
        self.wpar = False


class Op:
    __slots__ = ("eng", "fn", "deps", "idx", "dma", "sig", "rawdeps", "dsem", "dval", "fdeps")

    def __init__(self, eng, fn, idx, dma):
        self.eng = eng
        self.fn = fn
        self.idx = idx
        self.dma = dma
        self.deps = set()
        self.rawdeps = set()
        self.sig = None
        self.dsem = None
        self.dval = None
        self.fdeps = ()


class Prog:
    ENGS = ("pe", "act", "dve", "pool", "sp")

    def __init__(self, nc):
        self.nc = nc
        self.ops = []
        self.dma_count = {e: 0 for e in self.ENGS}
        self.dma_hist = {e: [] for e in self.ENGS}
        self.last = {e: None for e in self.ENGS}
        self.fence_deps = {e: set() for e in self.ENGS}

    def fence(self):
        deps = set()
        for e in self.ENGS:
            if self.last[e] is not None:
                deps.add(self.last[e])
            deps.update(self.dma_hist[e][-NDMASEM:])
        for e in self.ENGS:
            self.fence_deps[e] = set(deps)

    def eng_handle(self, e):
        nc = self.nc
        return {"pe": nc.tensor, "act": nc.scalar, "dve": nc.vector,
                "pool": nc.gpsimd, "sp": nc.sync}[e]

    def add(self, eng, fn, reads=(), writes=(), dma=False, par=False):
        idx = len(self.ops)
        op = Op(eng, fn, idx, dma)
        for t in reads:
            for w in t.w:
                op.deps.add(w)
                op.rawdeps.add(w)
        for t in writes:
            if not (par and t.wpar):
                for w in t.w:
                    op.deps.add(w)
            for r in t.r:
                op.deps.add(r)
        for t in reads:
            t.r.append(idx)
        for t in writes:
            if par and t.wpar:
                t.w.append(idx)
            else:
                t.w = [idx]
                t.wpar = par
                t.r = []
        if dma:
            j = self.dma_count[eng]
            self.dma_count[eng] = j + 1
            op.dsem = j % NDMASEM
            op.dval = 16 * (j // NDMASEM + 1)
            hist = self.dma_hist[eng]
            if j >= NDMASEM:
                op.deps.add(hist[j - NDMASEM])
            hist.append(idx)
        if self.fence_deps[eng]:
            op.deps.update(self.fence_deps[eng])
            op.fdeps = set(self.fence_deps[eng])
            self.fence_deps[eng] = set()
        if not dma:
            self.last[eng] = idx
        op.deps.discard(idx)
        self.ops.append(op)
        return op

    def pe(self, fn, reads=(), writes=()):
        return self.add("pe", fn, reads, writes)

    def act(self, fn, reads=(), writes=()):
        return self.add("act", fn, reads, writes)

    def dve(self, fn, reads=(), writes=()):
        return self.add("dve", fn, reads, writes)

    def pool(self, fn, reads=(), writes=()):
        return self.add("pool", fn, reads, writes)

    def dma(self, fn, reads=(), writes=(), q="sp", par=True):
        return self.add(q, fn, reads, writes, dma=True, par=par)

    def emit(self, stack):
        nc = self.nc
        ops = self.ops
        need = []
        for op in ops:
            nd = []
            for d in op.deps:
                dop = ops[d]
                if dop.dma or op.dma:
                    nd.append(d)
                elif dop.eng == op.eng:
                    if op.eng != "pe" or d in op.fdeps:
                        nd.append(d)
                else:
                    nd.append(d)
            need.append(nd)
        signaled = set()
        for nd in need:
            for d in nd:
                if not ops[d].dma:
                    signaled.add(d)
        cnt = {e: 0 for e in self.ENGS}
        for op in ops:
            if op.idx in signaled:
                cnt[op.eng] += 1
                op.sig = cnt[op.eng]
        csem = {e: stack.enter_context(nc.semaphore(f"c_{e}")) for e in self.ENGS if e != "sp"}
        dsem = {}
        for e in self.ENGS:
            if self.dma_count[e] > 0:
                dsem[e] = [stack.enter_context(nc.semaphore(f"d_{e}_{i}")) for i in range(NDMASEM)]
        seen = {e: {} for e in self.ENGS}
        nwait = 0
        self.dump = None
        for op in ops:
            h = self.eng_handle(op.eng)
            sn = seen[op.eng]
            w = {}
            for d in need[op.idx]:
                dop = ops[d]
                if dop.dma:
                    key = ("d", dop.eng, dop.dsem)
                    val = dop.dval
                else:
                    key = ("c", dop.eng)
                    val = dop.sig
                if val > w.get(key, 0):
                    w[key] = val
            for key, val in w.items():
                if sn.get(key, 0) >= val:
                    continue
                sn[key] = val
                sem = csem[key[1]] if key[0] == "c" else dsem[key[1]][key[2]]
                h.wait_ge(sem, val)
                nwait += 1
            if self.dump is not None:
                self.dump.append((op.idx, op.eng, op.dma, dict(w), op.sig, (op.dsem, op.dval) if op.dma else None))
            ins = op.fn(h)
            if op.dma:
                ins.then_inc(dsem[op.eng][op.dsem], 16)
            elif op.sig is not None:
                ins.then_inc(csem[op.eng], 1)
        for e in self.ENGS:
            n = self.dma_count[e]
            if n == 0:
                continue
            h = self.eng_handle(e)
            for i in range(NDMASEM):
                k = (n - i + NDMASEM - 1) // NDMASEM
                if k > 0:
                    h.wait_ge(dsem[e][i], 16 * k)
        self.nwait = nwait
        self.cnt = cnt
        return nwait


from contextlib import ExitStack
import math
import ml_dtypes
from concourse.bass_utils import run_bass_kernel_spmd

D = 1024
DFF = 2816
NF = DFF // 128
EPS = 1e-6
TT = 512
L_DEPTH = 2
NEGB = -30720.0
NWA = 2624
O_CQ, O_CKV, O_KR, O_DQ, O_DK, O_DV, O_NQ, O_NK, O_NVC, O_NVS, O_NG = (
    0, 256, 384, 480, 768, 1056, 1312, 1824, 2208, 2336, 2592)


NSW = 1568
SW_KR, SW_DQ, SW_DK, SW_NQ, SW_NK = 0, 96, 384, 672, 1184


class G:
    pass


def M(P, out, lhsT, rhs, start, stop, rd, wr):
    P.pe(lambda e: e.matmul(out, lhsT=lhsT, rhs=rhs, start=start, stop=stop), rd, wr)


def nextps(g):
    r = g.ps[g.pi % len(g.ps)]
    g.pi += 1
    return r


def rms_rstd(P, g, chunks, sq, t_sq, nfeat, rstd, t_rstd, rows=128):
    n = len(chunks)
    for c, (ap, tk) in enumerate(chunks):
        P.pool(lambda e, c=c, ap=ap: e.tensor_tensor(out=sq[0:rows, c, :], in0=ap, in1=ap, op=ALU.mult),
               reads=[tk], writes=[t_sq[c]])
    pst, pt = nextps(g)
    for c in range(n):
        M(P, pst[:, :], g.ones_bf[0:rows, :], sq[0:rows, c, :], c == 0, c == n - 1, [t_sq[c], g.t_const], [pt])
    P.act(lambda e: e.activation(out=rstd, in_=pst[:, :], func=AF.Sqrt, bias=g.eps[:, 0:1], scale=1.0 / nfeat),
          reads=[pt, g.t_const], writes=[t_rstd])
    P.dve(lambda e: e.reciprocal(out=rstd, in_=rstd), reads=[t_rstd], writes=[t_rstd])


def ffn_pass(P, nc, g, T, x_src, x_dst, g_dram, wg_d, wu_d, wd_d, fin_g=None, out_dst=None):
    P.fence()
    with ExitStack() as s:
        wg = s.enter_context(nc.sbuf_tensor(g.nm("wg"), [128, 8, DFF], BF16))
        wu = s.enter_context(nc.sbuf_tensor(g.nm("wu"), [128, 8, DFF], BF16))
        wd = s.enter_context(nc.sbuf_tensor(g.nm("wd"), [128, NF, D], BF16))
        xt = s.enter_context(nc.sbuf_tensor(g.nm("xt"), [128, 8, TT], F32))
        ht = s.enter_context(nc.sbuf_tensor(g.nm("ht"), [128, 8, TT], BF16))
        at = s.enter_context(nc.sbuf_tensor(g.nm("at"), [128, NF, TT], BF16))
        rstd = s.enter_context(nc.sbuf_tensor(g.nm("rstd"), [128, TT], F32))
        sg = [s.enter_context(nc.sbuf_tensor(g.nm("sg"), [128, TT], F32)) for i in range(2)]
        gt = s.enter_context(nc.sbuf_tensor(g.nm("gt"), [128, 16], F32))
        t_wg = [Tok() for _ in range(8)]
        t_wu = [Tok() for _ in range(8)]
        t_wd = [Tok() for _ in range(NF)]
        t_x, t_h, t_rstd, t_g = Tok(), Tok(), Tok(), Tok()
        t_a = [Tok() for _ in range(NF)]
        t_sg = [Tok(), Tok()]
        P.dma(lambda e: e.dma_start(out=gt[:, 0:8], in_=g_dram), writes=[t_g])
        if fin_g is not None:
            P.dma(lambda e: e.dma_start(out=gt[:, 8:16], in_=fin_g), writes=[t_g])
        for c in range(8):
            P.dma(lambda e, c=c: e.dma_start(out=wg[:, c, :], in_=wg_d[c * 128:(c + 1) * 128, :]),
                  writes=[t_wg[c]], q="pool")
            P.dma(lambda e, c=c: e.dma_start(out=wu[:, c, :], in_=wu_d[c * 128:(c + 1) * 128, :]),
                  writes=[t_wu[c]], q="pool")
        for f in range(NF):
            P.dma(lambda e, f=f: e.dma_start(out=wd[:, f, :], in_=wd_d[f * 128:(f + 1) * 128, :]),
                  writes=[t_wd[f]], q="pool")
        xs_v = x_src.rearrange("(c p) t -> p c t", p=128)
        xd_v = x_dst.rearrange("(c p) t -> p c t", p=128)
        for it in range(T // TT):
            tsl = slice(it * TT, (it + 1) * TT)
            P.dma(lambda e, tsl=tsl: e.dma_start(out=xt[:], in_=xs_v[:, :, tsl]),
                  reads=[g.t_x[it]], writes=[t_x])
            rms_rstd(P, g, [(xt[:, c, :], t_x) for c in range(8)], at, t_a, D, rstd[:], t_rstd)
            for c in range(8):
                P.dve(lambda e, c=c: e.scalar_tensor_tensor(out=ht[:, c, :], in0=xt[:, c, :], scalar=gt[:, c:c + 1], in1=rstd[:], op0=ALU.mult, op1=ALU.mult),
                      reads=[t_x, t_rstd, t_g], writes=[t_h])
            for f in range(NF):
                pg_, tg_ = nextps(g)
                pu_, tu_ = nextps(g)
                fs = slice(f * 128, (f + 1) * 128)
                for c in range(8):
                    M(P, pg_[:, :], wg[:, c, fs], ht[:, c, :], c == 0, c == 7, [t_wg[c], t_h], [tg_])
                for c in range(8):
                    M(P, pu_[:, :], wu[:, c, fs], ht[:, c, :], c == 0, c == 7, [t_wu[c], t_h], [tu_])
                sgi = f % 2
                P.act(lambda e, pg_=pg_, sgi=sgi: e.activation(out=sg[sgi][:], in_=pg_[:, :], func=AF.Silu),
                      reads=[tg_], writes=[t_sg[sgi]])
                P.dve(lambda e, pu_=pu_, sgi=sgi, f=f: e.tensor_tensor(out=at[:, f, :], in0=sg[sgi][:], in1=pu_[:, :], op=ALU.mult),
                      reads=[tu_, t_sg[sgi]], writes=[t_a[f]])
            for c in range(8):
                py_, ty_ = nextps(g)
                cs = slice(c * 128, (c + 1) * 128)
                for f in range(NF):
                    M(P, py_[:, :], wd[:, f, cs], at[:, f, :], f == 0, f == NF - 1, [t_wd[f], t_a[f]], [ty_])
                P.dve(lambda e, c=c, py_=py_: e.scalar_tensor_tensor(out=xt[:, c, :], in0=py_[:, :], scalar=0.5, in1=xt[:, c, :], op0=ALU.mult, op1=ALU.add),
                      reads=[ty_, t_x], writes=[t_x])
            if fin_g is None:
                P.dma(lambda e, tsl=tsl: e.dma_start(out=xd_v[:, :, tsl], in_=xt[:]), reads=[t_x], writes=[g.t_x[it]])
            else:
                rms_rstd(P, g, [(xt[:, c, :], t_x) for c in range(8)], at, t_a, D, rstd[:], t_rstd)
                for c in range(8):
                    P.dve(lambda e, c=c: e.scalar_tensor_tensor(out=xt[:, c, :], in0=xt[:, c, :], scalar=gt[:, 8 + c:9 + c], in1=rstd[:], op0=ALU.mult, op1=ALU.mult),
                          reads=[t_x, t_rstd, t_g], writes=[t_x])
                od_v = out_dst.rearrange("(c p) t -> p c t", p=128)
                P.dma(lambda e, tsl=tsl: e.dma_start(out=od_v[:, :, tsl], in_=xt[:]), reads=[t_x], writes=[g.t_out])


def proj_pass(P, nc, g, T, l, W):
    P.fence()
    S = g.S
    with ExitStack() as s:
        sb = lambda name, shape, dt: s.enter_context(nc.sbuf_tensor(g.nm(name), shape, dt))
        win = sb("win", [128, 8, NWA], BF16)
        wuq = sb("wuq", [128, 2, 384], BF16)
        wukv = sb("wukv", [128, 640], BF16)
        wsw = sb("wsw", [128, 8, NSW], BF16)
        wuqs = sb("wuqs", [128, 2, 384], BF16)
        gt = sb("gt", [128, 8], F32)
        qng = sb("qng", [128, 2], F32)
        kvng = sb("kvng", [128, 1], F32)
        gateb = sb("gateb", [128, 24], F32)
        xt = sb("xt", [128, 8, TT], F32)
        ht = sb("ht", [128, 8, TT], BF16)
        sq = sb("sq", [128, 8, TT], BF16)
        rstd = sb("rstd", [128, TT], F32)
        rstd2 = sb("rstd2", [128, TT], F32)
        cqf = sb("cqf", [128, 2, TT], F32)
        cqn = sb("cqn", [128, 2, TT], BF16)
        ckvf = sb("ckvf", [128, 1, TT], F32)
        ckvn = sb("ckvn", [128, TT], BF16)
        tabs = {k: sb("tab_" + k, [128, TT], F32) for k in ("mc", "ms", "dc", "ds", "nc", "ns")}
        t1 = [sb("t1", [128, TT], F32) for _ in range(2)]
        t2 = [sb("t2", [128, TT], F32) for _ in range(2)]
        stg = sb("stg", [128, 22, TT], BF16)
        vstg = sb("vstg", [128, 4, 520], BF16)
        mvstg = sb("mvstg", [128, 4, 260], BF16)
        gstg = sb("gstg", [128, 4, 24], F32)
        t_w = Tok()
        t_x, t_h, t_rstd, t_rstd2, t_cqf, t_cqn, t_ckvf, t_ckvn, t_tab = (Tok() for _ in range(9))
        t_sq = [Tok() for _ in range(8)]
        t_xb = [Tok(), Tok()]
        t_t1 = [Tok(), Tok()]
        t_t2 = [Tok(), Tok()]
        t_stg = [Tok() for _ in range(22)]
        t_vstg, t_mvstg, t_gstg = Tok(), Tok(), Tok()
        for c in range(8):
            P.dma(lambda e, c=c: e.dma_start(out=win[:, c, :], in_=W["w_inA"][l, c * 128:(c + 1) * 128, :]), writes=[t_w], q="pool")
        for c in range(2):
            P.dma(lambda e, c=c: e.dma_start(out=wuq[:, c, :], in_=W["w_uq"][l, c * 128:(c + 1) * 128, :]), writes=[t_w], q="pool")
        P.dma(lambda e: e.dma_start(out=wukv[:], in_=W["w_ukvA"][l]), writes=[t_w], q="pool")
        for c in range(8):
            P.dma(lambda e, c=c: e.dma_start(out=wsw[:, c, :], in_=W["w_inS"][l, c * 128:(c + 1) * 128, :]), writes=[t_w], q="pool")
        for c in range(2):
            P.dma(lambda e, c=c: e.dma_start(out=wuqs[:, c, :], in_=W["w_uqS"][l, c * 128:(c + 1) * 128, :]), writes=[t_w], q="pool")
        P.dma(lambda e: e.dma_start(out=gt[:], in_=W["mix_g"][l]), writes=[t_w])
        P.dma(lambda e: e.dma_start(out=qng[:], in_=W["qn_g"][l]), writes=[t_w])
        P.dma(lambda e: e.dma_start(out=kvng[:], in_=W["kvn_g"][l]), writes=[t_w])
        P.dma(lambda e: e.dma_start(out=gateb[:], in_=W["gate_b"][l]), writes=[t_w])
        xs_v = g.xs.rearrange("(c p) t -> p c t", p=128)
        rot = [0]
        P.dve(lambda e: e.memset(vstg[:], 1.0), writes=[t_vstg])
        P.dve(lambda e: e.memset(mvstg[:], 1.0), writes=[t_mvstg])

        def rope(pst, ptok, p2, p2t, rows, ck, sk, dst, dtok):
            i = rot[0] % 2
            rot[0] += 1
            P.dve(lambda e: e.tensor_tensor(out=t1[i][0:rows, :], in0=pst[0:rows, :], in1=tabs[ck][0:rows, :], op=ALU.mult),
                  reads=[ptok, t_tab], writes=[t_t1[i]])
            P.dve(lambda e: e.tensor_tensor(out=t2[i][0:rows, :], in0=p2[0:rows, :], in1=tabs[sk][0:rows, :], op=ALU.mult),
                  reads=[p2t, t_tab], writes=[t_t2[i]])
            P.pool(lambda e: e.tensor_tensor(out=dst, in0=t1[i][0:rows, :], in1=t2[i][0:rows, :], op=ALU.add),
                   reads=[t_t1[i], t_t2[i]], writes=[dtok])

        for it in range(T // TT):
            tsl = slice(it * TT, (it + 1) * TT)
            P.dma(lambda e, tsl=tsl: e.dma_start(out=xt[:], in_=xs_v[:, :, tsl]), reads=[g.t_x[it]], writes=[t_x])
            for k, nm_, rows in (("mc", "rt_mla_c", 96), ("ms", "rt_mla_s", 96), ("dc", "rt_diff_c", 96),
                                 ("ds", "rt_diff_s", 96), ("nc", "rt_nsa_c", 128), ("ns", "rt_nsa_s", 128)):
                P.dma(lambda e, k=k, nm_=nm_, rows=rows, tsl=tsl: e.dma_start(out=tabs[k][0:rows, :], in_=W[nm_][:, tsl]), writes=[t_tab])
            rms_rstd(P, g, [(xt[:, c, :], t_x) for c in range(8)], sq, t_sq, D, rstd[:], t_rstd)
            for c in range(8):
                P.dve(lambda e, c=c: e.scalar_tensor_tensor(out=ht[:, c, :], in0=xt[:, c, :], scalar=gt[:, c:c + 1], in1=rstd[:], op0=ALU.mult, op1=ALU.mult),
                      reads=[t_x, t_rstd, t_w], writes=[t_h])

            def fm(col0, m, pst, ptok, start=True, stop=True, nck=8, wt=None):
                wt = win if wt is None else wt
                for c in range(nck):
                    M(P, pst[0:m, :], wt[:, c, col0:col0 + m], ht[:, c, :], start and c == 0, stop and c == nck - 1, [t_w, t_h], [ptok])

            import os
            STG = int(os.environ.get("PROJ_STAGE", "99"))
            if STG < 2:
                continue
            for j in range(2):
                pst, ptok = nextps(g)
                fm(O_CQ + j * 128, 128, pst, ptok)
                P.act(lambda e, j=j, pst=pst: e.copy(out=cqf[:, j, :], in_=pst[:, :]), reads=[ptok], writes=[t_cqf])
            rms_rstd(P, g, [(cqf[:, j, :], t_cqf) for j in range(2)], sq, t_sq, 256, rstd2[:], t_rstd2)
            for j in range(2):
                P.dve(lambda e, j=j: e.scalar_tensor_tensor(out=cqn[:, j, :], in0=cqf[:, j, :], scalar=qng[:, j:j + 1], in1=rstd2[:], op0=ALU.mult, op1=ALU.mult),
                      reads=[t_cqf, t_rstd2, t_w], writes=[t_cqn])
            pst, ptok = nextps(g)
            fm(O_CKV, 128, pst, ptok)
            P.act(lambda e, pst=pst: e.copy(out=ckvf[:, 0, :], in_=pst[:, :]), reads=[ptok], writes=[t_ckvf])
            rms_rstd(P, g, [(ckvf[:, 0, :], t_ckvf)], sq, t_sq, 128, rstd2[:], t_rstd2)
            P.dve(lambda e: e.scalar_tensor_tensor(out=ckvn[:], in0=ckvf[:, 0, :], scalar=kvng[:, 0:1], in1=rstd2[:], op0=ALU.mult, op1=ALU.mult),
                  reads=[t_ckvf, t_rstd2, t_w], writes=[t_ckvn])
            gi = 0
            if STG < 3:
                continue
            for h in range(4):
                pst, ptok = nextps(g)
                p2, p2t = nextps(g)
                for j in range(2):
                    M(P, pst[0:96, :], wuq[:, j, h * 96:(h + 1) * 96], cqn[:, j, :], j == 0, j == 1, [t_w, t_cqn], [ptok])
                for j in range(2):
                    M(P, p2[0:96, :], wuqs[:, j, h * 96:(h + 1) * 96], cqn[:, j, :], j == 0, j == 1, [t_w, t_cqn], [p2t])
                rope(pst, ptok, p2, p2t, 96, "mc", "ms", stg[0:96, gi, :], t_stg[gi])
                P.dma(lambda e, h=h, gi=gi, tsl=tsl: e.dma_start(out=S["mla_qT"][h, :, tsl], in_=stg[0:96, gi, :]), reads=[t_stg[gi]], writes=[g.t_s["mla_qT"][it]])
                gi += 1
            for h in range(4):
                pst, ptok = nextps(g)
                p2, p2t = nextps(g)
                M(P, pst[0:96, :], wukv[:, h * 96:(h + 1) * 96], ckvn[:], True, False, [t_w, t_ckvn], [ptok])
                fm(O_KR, 96, pst, ptok, start=False, stop=True)
                fm(SW_KR, 96, p2, p2t, wt=wsw)
                rope(pst, ptok, p2, p2t, 96, "mc", "ms", stg[0:96, gi, :], t_stg[gi])
                P.dma(lambda e, h=h, gi=gi, tsl=tsl: e.dma_start(out=S["mla_kT"][h, :, tsl], in_=stg[0:96, gi, :]), reads=[t_stg[gi]], writes=[g.t_s["mla_kT"][it]])
                gi += 1
            for nm_, off, offs in (("diff_qT", O_DQ, SW_DQ), ("diff_kT", O_DK, SW_DK)):
                for j in range(3):
                    pst, ptok = nextps(g)
                    p2, p2t = nextps(g)
                    fm(off + j * 96, 96, pst, ptok)
                    fm(offs + j * 96, 96, p2, p2t, wt=wsw)
                    rope(pst, ptok, p2, p2t, 96, "dc", "ds", stg[0:96, gi, :], t_stg[gi])
                    P.dma(lambda e, nm_=nm_, j=j, gi=gi, tsl=tsl: e.dma_start(out=S[nm_][j, :, tsl], in_=stg[0:96, gi, :]), reads=[t_stg[gi]], writes=[g.t_s[nm_][it]])
                    gi += 1
            if STG < 6:
                continue
            for nm_, off, offs, n in (("nsa_qT", O_NQ, SW_NQ, 4), ("nsa_kT", O_NK, SW_NK, 3)):
                for j in range(n):
                    pst, ptok = nextps(g)
                    p2, p2t = nextps(g)
                    fm(off + j * 128, 128, pst, ptok)
                    fm(offs + j * 128, 128, p2, p2t, wt=wsw)
                    rope(pst, ptok, p2, p2t, 128, "nc", "ns", stg[:, gi, :], t_stg[gi])
                    P.dma(lambda e, nm_=nm_, j=j, gi=gi, tsl=tsl: e.dma_start(out=S[nm_][j, :, tsl], in_=stg[:, gi, :]), reads=[t_stg[gi]], writes=[g.t_s[nm_][it]])
                    gi += 1
            if STG < 7:
                continue
            pst, ptok = nextps(g)
            fm(O_NVC, 128, pst, ptok)
            P.act(lambda e, pst=pst, gi=gi: e.copy(out=stg[:, gi, :], in_=pst[:, :]), reads=[ptok], writes=[t_stg[gi]])
            P.dma(lambda e, gi=gi, tsl=tsl: e.dma_start(out=S["nsa_vcT"][:, tsl], in_=stg[:, gi, :]), reads=[t_stg[gi]], writes=[g.t_s["nsa_vcT"][it]])
            gi += 1
            if STG < 8:
                continue
            TMS = os.environ.get("TM_SKIP", "")
            for s_ in range(4):
                ssl = slice(s_ * 128, (s_ + 1) * 128)
                if "v" not in TMS:
                    pst, ptok = nextps(g)
                    for c in range(8):
                        M(P, pst[:, 0:256], ht[:, c, ssl], win[:, c, O_DV:O_DV + 256], c == 0, c == 7, [t_w, t_h], [ptok])
                    for c in range(8):
                        M(P, pst[:, 256:512], ht[:, c, ssl], win[:, c, O_NVS:O_NVS + 256], c == 0, c == 7, [t_w, t_h], [ptok])
                    P.act(lambda e, pst=pst, s_=s_: e.copy(out=vstg[:, s_, :].rearrange("p (j d) -> p j d", d=65)[:, :, 0:64], in_=pst[:, :].rearrange("p (j d) -> p j d", d=64)),
                          reads=[ptok], writes=[t_vstg])
                if "m" not in TMS:
                    pst, ptok = nextps(g)
                    M(P, pst[:, 0:256], ckvn[:, ssl], wukv[:, 384:640], True, True, [t_w, t_ckvn], [ptok])
                    P.act(lambda e, pst=pst, s_=s_: e.copy(out=mvstg[:, s_, :].rearrange("p (j d) -> p j d", d=65)[:, :, 0:64], in_=pst[:, 0:256].rearrange("p (j d) -> p j d", d=64)),
                          reads=[ptok], writes=[t_mvstg])
                if "g" not in TMS:
                    pst, ptok = nextps(g)
                    for c in range(8):
                        M(P, pst[:, 0:24], ht[:, c, ssl], win[:, c, O_NG:O_NG + 24], c == 0, c == 7, [t_w, t_h], [ptok])
                    P.dve(lambda e, pst=pst, s_=s_: e.tensor_tensor(out=gstg[:, s_, :], in0=pst[:, 0:24], in1=gateb[:], op=ALU.add),
                          reads=[ptok, t_w], writes=[t_gstg])
            if "g" not in TMS:
                P.dma(lambda e, tsl=tsl: e.dma_start(out=S["gates"][tsl, :].rearrange("(s p) c -> p s c", p=128), in_=gstg[:]), reads=[t_gstg], writes=[g.t_s["gates"][it]])
            if "v" not in TMS:
                P.dma(lambda e, tsl=tsl: e.dma_start(out=S["v_tm"][tsl, :].rearrange("(s p) c -> p s c", p=128), in_=vstg[:]), reads=[t_vstg], writes=[g.t_s["v_tm"][it]])
            if "m" not in TMS:
                P.dma(lambda e, tsl=tsl: e.dma_start(out=S["mla_v"][tsl, :].rearrange("(s p) c -> p s c", p=128), in_=mvstg[:]), reads=[t_mvstg], writes=[g.t_s["mla_v"][it]])


class Blk:
    __slots__ = ("smm", "qlo", "qhi", "scale", "pv", "acctok", "post")


def mk_pv(chunk_specs, acc_view, width, v_fn):
    out = []
    n = len(chunk_specs)
    for idx, (c, slo, shi) in enumerate(chunk_specs):
        vap, vtok = v_fn(c)
        lst = []
        for s_ in range(slo, shi):
            st = (idx == 0 and s_ == slo)
            sp = (idx == n - 1 and s_ == shi - 1)
            lst.append((acc_view[:, s_, 0:width], s_, vap, vtok, st, sp))
        out.append(lst)
    return out


def causal_chunks(i):
    import os
    res = []
    for c in range(4 * i):
        if os.environ.get("DIAGONLY"):
            continue
        res.append((c, None, 0, 0, 512, 0, 4))
    for o in range(4):
        res.append((4 * i + o, "c", o, 128 * o, 512, o, 4))
    return res


def win_chunks(i):
    res = []
    for o in range(4):
        c = 4 * i - 4 + o
        if c >= 0:
            res.append((c, "l", o, 0, 128 * (o + 1), 0, o + 1))
    for o in range(4):
        res.append((4 * i + o, "c", o, 128 * o, 512, o, 4))
    return res


def mla_pass(P, nc, g, T, l, W):
    P.fence()
    S = g.S
    NC_ = T // 128
    NQB = T // TT
    scale = 96 ** -0.5
    with ExitStack() as s:
        sb = lambda name, shape, dt: s.enter_context(nc.sbuf_tensor(g.nm(name), shape, dt))
        KT = sb("KT", [96, 4, T], BF16)
        V = sb("V", [128, NC_, 260], BF16)
        Q = [sb("Q", [96, 4, TT], BF16) for _ in range(2)]
        pts = [sb("pt", [128, TT], BF16) for _ in range(3)]
        otile = sb("otile", [128, 4, 256], BF16)
        otile2 = sb("otile2", [128, 4, 256], BF16)
        oT = sb("oT", [128, 2, TT], BF16)
        rl = sb("rl", [128, 4], F32)
        t_K, t_V, t_rl, t_ot, t_oT, t_ot2 = Tok(), Tok(), Tok(), Tok(), Tok(), Tok()
        t_Q = [Tok(), Tok()]
        t_pts = [Tok() for _ in range(3)]
        for h in range(4):
            P.dma(lambda e, h=h: e.dma_start(out=KT[:, h, :], in_=S["mla_kT"][h]), reads=g.t_s["mla_kT"], writes=[t_K])
        for c0 in range(0, NC_, 4):
            P.dma(lambda e, c0=c0: e.dma_start(out=V[:, c0:c0 + 4, :], in_=S["mla_v"][c0 * 128:(c0 + 4) * 128, :].rearrange("(c p) d -> p c d", p=128)),
                  reads=g.t_s["mla_v"], writes=[t_V])
        sbanks = g.ps[0:3]
        accs = g.ps[3:5]

        def load_q(i):
            if i >= NQB:
                return
            tsl = slice(i * TT, (i + 1) * TT)
            for h in range(4):
                P.dma(lambda e, h=h: e.dma_start(out=Q[i % 2][:, h, :], in_=S["mla_qT"][h, :, tsl]), reads=[g.t_s["mla_qT"][i]], writes=[t_Q[i % 2]])

        hcount = 0
        blks = []
        k2ld = {}
        load_q(0)
        for i in range(NQB):
            qb = i % 2
            tsl = slice(i * TT, (i + 1) * TT)
            k2ld[len(blks)] = (lambda i=i: load_q(i + 1))
            specs = causal_chunks(i)
            for h in range(4):
                acc, acctok = accs[hcount % 2]
                hcount += 1
                accv = acc[:, :].rearrange("p (s c) -> p s c", c=128)
                pvs = mk_pv([(c, slo, shi) for (c, kd, o, qlo, qhi, slo, shi) in specs], accv, 65,
                            lambda c, h=h: (V[:, c, h * 65:(h + 1) * 65], t_V))
                for idx, (c, kd, o, qlo, qhi, slo, shi) in enumerate(specs):
                    b = Blk()
                    b.smm = [(KT[:, h, c * 128:(c + 1) * 128], Q[qb][:, h, qlo:qhi], [t_K, t_Q[qb]])]
                    if kd == "c":
                        b.smm.append((g.ident_bf[:, :], g.cbias[:, o, qlo:qhi], [g.t_const]))
                    b.qlo, b.qhi, b.scale = qlo, qhi, scale
                    b.pv = pvs[idx]
                    b.acctok = acctok
                    b.post = None
                    if idx == len(specs) - 1:
                        def post(h=h, accv=accv, acctok=acctok, tsl=tsl, i=i):
                            P.dve(lambda e: e.reciprocal(out=rl[:, :], in_=accv[:, :, 64]), reads=[acctok], writes=[t_rl])
                            for s_ in range(4):
                                P.dve(lambda e, s_=s_: e.tensor_scalar(out=otile[:, s_, h * 64:(h + 1) * 64], in0=accv[:, s_, 0:64], scalar1=rl[:, s_:s_ + 1], scalar2=None, op0=ALU.mult),
                                      reads=[acctok, t_rl], writes=[t_ot])
                            if h == 3:
                                def pe_part():
                                    import os
                                    if os.environ.get("NO_TR"):
                                        return
                                    for hf in range(2):
                                        for s_ in range(4):
                                            P.pe(lambda e, hf=hf, s_=s_: e.transpose(out=g.psT[:, hf, s_ * 128:(s_ + 1) * 128], in_=otile2[:, s_, hf * 128:(hf + 1) * 128], identity=g.ident_bf[:, :]),
                                                 reads=[t_ot2, g.t_const], writes=[g.t_psT])
                                    P.act(lambda e: e.copy(out=oT[:, :, :], in_=g.psT[:, :, :]), reads=[g.t_psT], writes=[t_oT])
                                    for hf in range(2):
                                        P.dma(lambda e, hf=hf: e.dma_start(out=S["oT"][hf * 128:(hf + 1) * 128, tsl], in_=oT[:, hf, :]), reads=[t_oT], writes=[g.t_s["oT"][i]])
                                P.pool(lambda e: e.tensor_copy(out=otile2[:, :, :], in_=otile[:, :, :]), reads=[t_ot], writes=[t_ot2])
                                return pe_part
                        b.post = post
                    blks.append(b)
        run_blocks_ld(P, g, blks, sbanks, pts, t_pts, k2ld)


def run_blocks_ld(P, g, blks, sbanks, pts, t_pts, k2ld):
    import os
    n = len(blks)
    nsb = len(sbanks)
    npt = len(pts)

    def S_(k):
        if k in k2ld:
            k2ld[k]()
        b = blks[k]
        bank, btok = sbanks[k % nsb]
        if len(b.smm) == 1 and os.environ.get("ZPAD", "0") == "1":
            b.smm.append((g.ident_bf[:, :], g.zbias[:, b.qlo:b.qhi], [g.t_const]))
        m = len(b.smm)
        for j, (lt, rh, rd) in enumerate(b.smm):
            M(P, bank[:, b.qlo:b.qhi], lt, rh, j == 0, j == m - 1, rd, [btok])

    def E_(k):
        b = blks[k]
        bank, btok = sbanks[k % nsb]
        i = k % npt
        P.act(lambda e: e.activation(out=pts[i][:, b.qlo:b.qhi], in_=bank[:, b.qlo:b.qhi], func=AF.Exp, scale=b.scale),
              reads=[btok], writes=[t_pts[i]])

    def V_(k):
        b = blks[k]
        i = k % npt
        for (out, s_, vap, vtok, st, sp) in b.pv:
            M(P, out, pts[i][:, s_ * 128:(s_ + 1) * 128], vap, st, sp, [t_pts[i], vtok], [b.acctok])
        if b.post is not None:
            r = b.post()
            if r is not None:
                for fn in (r if isinstance(r, (list, tuple)) else [r]):
                    deferred.append((k + DEFER, fn))
        while deferred and deferred[0][0] <= k:
            deferred.pop(0)[1]()

    DEFER = 3
    deferred = []
    if n == 0:
        return
    S_(0)
    if n > 1:
        S_(1)
    for k in range(n):
        E_(k)
        V_(k)
        if k + 2 < n:
            S_(k + 2)
    while deferred:
        deferred.pop(0)[1]()


def diff_pass(P, nc, g, T, l, W, lam_init):
    P.fence()
    S = g.S
    NC_ = T // 128
    NQB = T // TT
    scale = 32 ** -0.5
    with ExitStack() as s:
        sb = lambda name, shape, dt: s.enter_context(nc.sbuf_tensor(g.nm(name), shape, dt))
        KT = sb("KT", [96, 3, T], BF16)
        V = sb("V", [128, NC_, 260], BF16)
        Q = [sb("Q", [96, 3, TT], BF16) for _ in range(2)]
        pts = [sb("pt", [128, TT], BF16) for _ in range(3)]
        otile = sb("otile", [128, 4, 256], BF16)
        otile2 = sb("otile2", [128, 4, 256], BF16)
        t_ot2 = Tok()
        oT = sb("oT", [128, 2, TT], BF16)
        rl = sb("rl", [128, 2, 4], F32)
        o1 = sb("o1", [128, 4, 64], F32)
        dd = sb("dd", [128, 4, 64], F32)
        sqd = sb("sqd", [128, 4, 64], F32)
        ssq = sb("ssq", [128, 4], F32)
        dl = sb("dl", [128, 128], F32)
        gv = sb("gv", [128, 64], F32)
        lamt = sb("lamt", [128, 8], F32)
        t_K, t_V, t_rl, t_ot, t_oT, t_o1, t_dd, t_sqd, t_ssq, t_c = (Tok() for _ in range(10))
        t_Q = [Tok(), Tok()]
        t_pts = [Tok() for _ in range(3)]
        for j in range(3):
            P.dma(lambda e, j=j: e.dma_start(out=KT[:, j, :], in_=S["diff_kT"][j]), reads=g.t_s["diff_kT"], writes=[t_K])
        for c0 in range(0, NC_, 4):
            P.dma(lambda e, c0=c0: e.dma_start(out=V[:, c0:c0 + 4, :], in_=S["v_tm"][c0 * 128:(c0 + 4) * 128, 0:260].rearrange("(c p) d -> p c d", p=128)),
                  reads=g.t_s["v_tm"], writes=[t_V])
        P.dma(lambda e: e.dma_start(out=dl[:], in_=W["dlam"][l]), writes=[t_c])
        P.dma(lambda e: e.dma_start(out=gv[:], in_=W["dng"][l]), writes=[t_c])
        P.dve(lambda e: e.tensor_tensor(out=dl[:, 0:32], in0=dl[:, 0:32], in1=dl[:, 32:64], op=ALU.mult), reads=[t_c], writes=[t_c])
        P.dve(lambda e: e.tensor_tensor(out=dl[:, 64:96], in0=dl[:, 64:96], in1=dl[:, 96:128], op=ALU.mult), reads=[t_c], writes=[t_c])
        P.dve(lambda e: e.reduce_sum(out=lamt[:, 0:1], in_=dl[:, 0:32], axis=AX.X), reads=[t_c], writes=[t_c])
        P.dve(lambda e: e.reduce_sum(out=lamt[:, 1:2], in_=dl[:, 64:96], axis=AX.X), reads=[t_c], writes=[t_c])
        P.act(lambda e: e.activation(out=lamt[:, 2:4], in_=lamt[:, 0:2], func=AF.Exp), reads=[t_c], writes=[t_c])
        P.dve(lambda e: e.tensor_tensor(out=lamt[:, 4:5], in0=lamt[:, 3:4], in1=lamt[:, 2:3], op=ALU.subtract), reads=[t_c], writes=[t_c])
        P.dve(lambda e: e.tensor_scalar(out=lamt[:, 4:5], in0=lamt[:, 4:5], scalar1=-lam_init, scalar2=None, op0=ALU.add), reads=[t_c], writes=[t_c])
        P.dve(lambda e: e.tensor_scalar(out=gv[:], in0=gv[:], scalar1=1.0 - lam_init, scalar2=None, op0=ALU.mult), reads=[t_c], writes=[t_c])
        sbanks = g.ps[0:3]
        accs = g.ps[3:7]

        def load_q(i):
            if i >= NQB:
                return
            tsl = slice(i * TT, (i + 1) * TT)
            for j in range(3):
                P.dma(lambda e, j=j: e.dma_start(out=Q[i % 2][:, j, :], in_=S["diff_qT"][j, :, tsl]), reads=[g.t_s["diff_qT"][i]], writes=[t_Q[i % 2]])

        pcount = 0
        blks = []
        k2ld = {}
        load_q(0)
        for i in range(NQB):
            qb = i % 2
            tsl = slice(i * TT, (i + 1) * TT)
            k2ld[len(blks)] = (lambda i=i: load_q(i + 1))
            specs = causal_chunks(i)
            for h in range(4):
                accp = []
                for m in range(2):
                    p = 2 * h + m
                    j, base = p // 3, 32 * (p % 3)
                    acc, acctok = accs[pcount % 4]
                    pcount += 1
                    accv = acc[:, :].rearrange("p (s c) -> p s c", c=128)
                    accp.append((accv, acctok))
                    pvs = mk_pv([(c, slo, shi) for (c, kd, o, qlo, qhi, slo, shi) in specs], accv, 65,
                                lambda c, h=h: (V[:, c, h * 65:(h + 1) * 65], t_V))
                    for idx, (c, kd, o, qlo, qhi, slo, shi) in enumerate(specs):
                        b = Blk()
                        b.smm = [(KT[base:base + 32, j, c * 128:(c + 1) * 128], Q[qb][base:base + 32, j, qlo:qhi], [t_K, t_Q[qb]])]
                        if kd == "c":
                            b.smm.append((g.ident_bf[:, :], g.cbias[:, o, qlo:qhi], [g.t_const]))
                        b.qlo, b.qhi, b.scale = qlo, qhi, scale
                        b.pv = pvs[idx]
                        b.acctok = acctok
                        b.post = None
                        if idx == len(specs) - 1 and m == 1:
                            def post(h=h, accp=list(accp), tsl=tsl, i=i):
                                (a1, k1), (a2, k2) = accp
                                P.dve(lambda e: e.reciprocal(out=rl[:, 0, :], in_=a1[:, :, 64]), reads=[k1], writes=[t_rl])
                                P.dve(lambda e: e.reciprocal(out=rl[:, 1, :], in_=a2[:, :, 64]), reads=[k2], writes=[t_rl])
                                P.dve(lambda e: e.tensor_scalar(out=rl[:, 1, :], in0=rl[:, 1, :], scalar1=lamt[:, 4:5], scalar2=None, op0=ALU.mult), reads=[t_rl, t_c], writes=[t_rl])
                                for s_ in range(4):
                                    P.dve(lambda e, s_=s_: e.tensor_scalar(out=o1[:, s_, :], in0=a1[:, s_, 0:64], scalar1=rl[:, 0, s_:s_ + 1], scalar2=None, op0=ALU.mult),
                                          reads=[k1, t_rl], writes=[t_o1])
                                    P.dve(lambda e, s_=s_: e.scalar_tensor_tensor(out=dd[:, s_, :], in0=a2[:, s_, 0:64], scalar=rl[:, 1, s_:s_ + 1], in1=o1[:, s_, :], op0=ALU.mult, op1=ALU.add),
                                          reads=[k2, t_rl, t_o1], writes=[t_dd])
                                P.pool(lambda e: e.tensor_tensor(out=sqd[:], in0=dd[:], in1=dd[:], op=ALU.mult), reads=[t_dd], writes=[t_sqd])
                                P.dve(lambda e: e.reduce_sum(out=ssq[:], in_=sqd[:], axis=AX.X), reads=[t_sqd], writes=[t_ssq])
                                P.act(lambda e: e.activation(out=ssq[:], in_=ssq[:], func=AF.Ln, bias=g.eps[:, 0:1], scale=1.0 / 64), reads=[t_ssq, g.t_const], writes=[t_ssq])
                                P.act(lambda e: e.activation(out=ssq[:], in_=ssq[:], func=AF.Exp, scale=-0.5), reads=[t_ssq], writes=[t_ssq])
                                for s_ in range(4):
                                    P.dve(lambda e, s_=s_: e.scalar_tensor_tensor(out=otile[:, s_, h * 64:(h + 1) * 64], in0=dd[:, s_, :], scalar=ssq[:, s_:s_ + 1], in1=gv[:], op0=ALU.mult, op1=ALU.mult),
                                          reads=[t_dd, t_ssq, t_c], writes=[t_ot])
                                if h == 3:
                                    def pe_part():
                                        for hf in range(2):
                                            for s_ in range(4):
                                                P.pe(lambda e, hf=hf, s_=s_: e.transpose(out=g.psT[:, hf, s_ * 128:(s_ + 1) * 128], in_=otile2[:, s_, hf * 128:(hf + 1) * 128], identity=g.ident_bf[:, :]),
                                                     reads=[t_ot2, g.t_const], writes=[g.t_psT])
                                        P.act(lambda e: e.copy(out=oT[:, :, :], in_=g.psT[:, :, :]), reads=[g.t_psT], writes=[t_oT])
                                        for hf in range(2):
                                            P.dma(lambda e, hf=hf: e.dma_start(out=S["oT"][256 + hf * 128:256 + (hf + 1) * 128, tsl], in_=oT[:, hf, :]), reads=[t_oT], writes=[g.t_s["oT"][i]])
                                    P.pool(lambda e: e.tensor_copy(out=otile2[:, :, :], in_=otile[:, :, :]), reads=[t_ot], writes=[t_ot2])
                                    return pe_part
                            b.post = post
                        blks.append(b)
        run_blocks_ld(P, g, blks, sbanks, pts, t_pts, k2ld)


def nsa_pass(P, nc, g, T, l, W):
    import os
    P.fence()
    S = g.S
    NC_ = T // 128
    NQB = T // TT
    NBS = T // 64
    NCMP = T // 16 - 1
    NCH = T // 2048
    NR = NBS - 1
    WC = 65 + NR
    scale = 0.125
    with ExitStack() as s:
        sb = lambda name, shape, dt: s.enter_context(nc.sbuf_tensor(g.nm(name), shape, dt))
        Kslc = sb("Kslc", [128, T], BF16)
        Kwin = sb("Kwin", [128, T], BF16)
        Vs = sb("Vs", [128, NC_, 260], BF16)
        KcT = sb("KcT", [128, NCH * 128], BF16)
        Vc = sb("Vc", [128, NCH, 2, 128], BF16)
        ebig = sb("ebig", [128, T], BF16)
        t_K, t_V, t_Kc, t_Vc, t_c = (Tok() for _ in range(5))
        P.dma(lambda e: e.dma_start(out=Kslc[:], in_=S["nsa_kT"][1]), reads=g.t_s["nsa_kT"], writes=[t_K])
        P.dma(lambda e: e.dma_start(out=Kwin[:], in_=S["nsa_kT"][2]), reads=g.t_s["nsa_kT"], writes=[t_K])
        for c0 in range(0, NC_, 4):
            P.dma(lambda e, c0=c0: e.dma_start(out=Vs[:, c0:c0 + 4, :], in_=S["v_tm"][c0 * 128:(c0 + 4) * 128, 260:520].rearrange("(c p) d -> p c d", p=128)),
                  reads=g.t_s["v_tm"], writes=[t_V])
        P.dma(lambda e: e.dma_start(out=ebig[:], in_=W["ebig"]), writes=[t_c])
        P.dve(lambda e: e.memset(KcT[:], 0.0), writes=[t_Kc])
        P.dve(lambda e: e.memset(Vc[:], 0.0), writes=[t_Vc])
        with ExitStack() as s2:
            sb2 = lambda name, shape, dt: s2.enter_context(nc.sbuf_tensor(g.nm(name), shape, dt))
            src = [sb2("csrc", [128, T], BF16) for _ in range(2)]
            w1 = [sb2("cw1", [128, 32, 256], BF16) for _ in range(2)]
            w2 = [sb2("cw2", [128, 2, 128], BF16) for _ in range(2)]
            posT = sb2("cpos", [128, 2, 32], BF16)
            b1 = sb2("cb1", [128, 2, 2], F32)
            btot = sb2("btot", [128, 2, 2], F32)
            t_src, t_cw, t_btot = Tok(), Tok(), Tok()
            P.dma(lambda e: e.dma_start(out=src[0][:], in_=S["nsa_kT"][0]), reads=g.t_s["nsa_kT"], writes=[t_src])
            P.dma(lambda e: e.dma_start(out=src[1][:], in_=S["nsa_vcT"]), reads=g.t_s["nsa_vcT"], writes=[t_src])
            for kv in range(2):
                P.dma(lambda e, kv=kv: e.dma_start(out=w1[kv][:], in_=W["cw1"][l, kv].rearrange("p (l f) -> p l f", f=256)), writes=[t_cw], q="pool")
                P.dma(lambda e, kv=kv: e.dma_start(out=w2[kv][:], in_=W["cw2"][l, kv].rearrange("p (c f) -> p c f", f=128)), writes=[t_cw], q="pool")
                P.dma(lambda e, kv=kv: e.dma_start(out=posT[:, kv, :], in_=W["cpos"][l, kv]), writes=[t_cw], q="pool")
                P.dma(lambda e, kv=kv: e.dma_start(out=b1[:, kv, :], in_=W["cb1"][l, kv]), writes=[t_cw])
            P.dma(lambda e: e.dma_start(out=Vc[:, :, 0, 65:65 + NR], in_=W["mapm"].rearrange("(c p) j -> p c j", p=128)), reads=[], writes=[t_Vc])
            P.dma(lambda e: e.dma_start(out=Vc[:, :, 1, 65:65 + NR], in_=W["mapm"].rearrange("(c p) j -> p c j", p=128)), reads=[], writes=[t_Vc])
            P.dve(lambda e: e.memset(Vc[:, :, :, 64:65], 1.0), reads=[], writes=[t_Vc])
            hids = {(kv, gg): sb2("hid", [128, 2, NCH * 128], BF16) for kv in range(2) for gg in range(2)}
            t_hids = {k_: Tok() for k_ in hids}
            for kv in range(2):
                for fc in range(2):
                    pst, ptok = nextps(g)
                    for ll in range(32):
                        M(P, pst[:, 0:1], w1[kv][0:64, ll, fc * 128:(fc + 1) * 128], posT[0:64, kv, ll:ll + 1], ll == 0, ll == 31, [t_cw], [ptok])
                    P.dve(lambda e, kv=kv, fc=fc, pst=pst: e.tensor_tensor(out=btot[:, kv, fc:fc + 1], in0=pst[:, 0:1], in1=b1[:, kv, fc:fc + 1], op=ALU.add),
                          reads=[ptok, t_cw], writes=[t_btot])
            for kv in range(2):
                for gg in range(2):
                    r0 = 64 * gg
                    hid_ = hids[(kv, gg)]
                    for fc in range(2):
                        pst, ptok = nextps(g)
                        for ll in range(32):
                            M(P, pst[:, 0:NCMP], w1[kv][r0:r0 + 64, ll, fc * 128:(fc + 1) * 128],
                              src[kv][r0:r0 + 64, ll:ll + 16 * (NCMP - 1) + 1:16], ll == 0, ll == 31, [t_cw, t_src], [ptok])
                        P.act(lambda e, kv=kv, fc=fc, pst=pst, hid_=hid_: e.activation(out=hid_[:, fc, 0:NCMP], in_=pst[:, 0:NCMP], func=AF.Silu, bias=btot[:, kv, fc:fc + 1]),
                              reads=[ptok, t_btot], writes=[t_hids[(kv, gg)]])
            for kv in range(2):
                for gg in range(2):
                    hid_ = hids[(kv, gg)]
                    t_hid = t_hids[(kv, gg)]
                    if kv == 0:
                        pst, ptok = nextps(g)
                        if gg == 0:
                            for fc in range(2):
                                M(P, pst[0:64, 0:NCMP], w2[0][:, fc, 64:128], hid_[:, fc, 0:NCMP], fc == 0, fc == 1, [t_cw, t_hid], [ptok])
                            P.act(lambda e, pst=pst: e.copy(out=KcT[0:64, 0:NCMP], in_=pst[0:64, 0:NCMP]), reads=[ptok], writes=[t_Kc])
                        else:
                            for fc in range(2):
                                M(P, pst[:, 0:NCMP], w2[0][:, fc, 0:128], hid_[:, fc, 0:NCMP], fc == 0, fc == 1, [t_cw, t_hid], [ptok])
                            P.act(lambda e, pst=pst: e.copy(out=KcT[64:128, 0:NCMP], in_=pst[64:128, 0:NCMP]), reads=[ptok], writes=[t_Kc])
                    else:
                        for nch in range(NCH):
                            m = min(128, NCMP - nch * 128)
                            pst, ptok = nextps(g)
                            for fc in range(2):
                                M(P, pst[0:m, 0:64], hid_[:, fc, nch * 128:nch * 128 + m], w2[1][:, fc, 64:128], fc == 0, fc == 1, [t_cw, t_hid], [ptok])
                            P.act(lambda e, pst=pst, m=m, nch=nch, gg=gg: e.copy(out=Vc[0:m, nch, gg, 0:64], in_=pst[0:m, 0:64]), reads=[ptok], writes=[t_Vc])
        P.fence()
        Q = [sb("Q", [128, 4, TT], BF16) for _ in range(2)]
        cmpb = [sb("cmpb", [128, NCH, TT], BF16) for _ in range(2)]
        gat = [sb("gat", [128, 4, 24], F32) for _ in range(3)]
        vmt = [sb("vmt", [128, 4, NBS], F32) for _ in range(3)]
        adt = [sb("adt", [128, 4, NBS], F32) for _ in range(3)]
        t_G = [Tok() for _ in range(3)]
        t_Gg = [Tok() for _ in range(3)]
        pts = [sb("pt", [128, TT], BF16) for _ in range(3)]
        ocomb = sb("ocomb", [128, 4, 512], F32)
        otile = sb("otile", [128, 4, 512], BF16)
        oT = sb("oT", [128, 4, TT], BF16)
        rl = sb("rl", [128, 4], F32)
        wgt = sb("wgt", [128, 4], F32)
        pslc = sb("pslc", [128, 2, 4, NBS], F32)
        score = sb("score", [128, 4, NBS], F32)
        work = sb("work", [128, NBS], F32)
        mx = sb("mx", [128, 16], F32)
        selm1 = [sb("selm1", [128, 4, NBS], F32) for _ in range(2)]
        selT = [sb("selT", [128, TT], BF16) for _ in range(2)]
        t_Q = [Tok(), Tok()]
        t_pts = [Tok() for _ in range(3)]
        t_oc, t_ot, t_oT, t_rl, t_wgt, t_score, t_work, t_mx = (Tok() for _ in range(8))
        t_selm1 = [Tok(), Tok()]
        t_pslc = [Tok(), Tok()]
        t_selT = [Tok(), Tok()]
        for gg_ in range(2):
            P.dve(lambda e, gg_=gg_: e.memset(selT[gg_][:], 0.0), writes=[t_selT[gg_]])
        sbanks = g.ps[0:3]
        accs = g.ps[3:5]
        misc = g.ps[5:7]

        def load_q(i):
            if i >= NQB:
                return
            tsl = slice(i * TT, (i + 1) * TT)
            b_ = i % 2
            for j in range(4):
                P.dma(lambda e, j=j: e.dma_start(out=Q[b_][:, j, :], in_=S["nsa_qT"][j, :, tsl]), reads=[g.t_s["nsa_qT"][i]], writes=[t_Q[b_]])
            P.dma(lambda e: e.dma_start(out=cmpb[b_][:], in_=W["cmpbias"][:, :, tsl].rearrange("c p t -> p c t")), writes=[t_Q[b_]])
            b3 = i % 3
            P.dma(lambda e: e.dma_start(out=gat[b3][:], in_=S["gates"][tsl, :].rearrange("(s p) c -> p s c", p=128)), reads=[g.t_s["gates"][i]], writes=[t_Gg[b3]])
            P.act(lambda e: e.activation(out=gat[b3][:], in_=gat[b3][:], func=AF.Exp, scale=-1.0), reads=[t_Gg[b3]], writes=[t_Gg[b3]])
            P.dve(lambda e: e.tensor_scalar(out=gat[b3][:], in0=gat[b3][:], scalar1=1.0, scalar2=None, op0=ALU.add), reads=[t_Gg[b3]], writes=[t_Gg[b3]])
            P.dve(lambda e: e.reciprocal(out=gat[b3][:], in_=gat[b3][:]), reads=[t_Gg[b3]], writes=[t_Gg[b3]])
            P.dma(lambda e: e.dma_start(out=vmt[b3][:], in_=W["validm"][tsl, :].rearrange("(s p) c -> p s c", p=128)), writes=[t_G[b3]])
            P.dma(lambda e: e.dma_start(out=adt[b3][:], in_=W["addc"][tsl, :].rearrange("(s p) c -> p s c", p=128)), writes=[t_G[b3]])

        hcount = 0
        blks = []
        k2ld = {}
        load_q(0)
        for i in range(NQB):
            qb = i % 2
            q3 = i % 3
            tsl = slice(i * TT, (i + 1) * TT)
            k2ld[len(blks)] = (lambda i=i: load_q(i + 1))

            def finish(tsl=tsl, i=i):
                P.act(lambda e: e.copy(out=otile[:, :, :], in_=ocomb[:, :, :]), reads=[t_oc], writes=[t_ot])

                def pe_part():
                    for pr in range(2):
                        for hh in range(2):
                            cb = pr * 2 + hh
                            for s_ in range(4):
                                P.pe(lambda e, cb=cb, hh=hh, s_=s_: e.transpose(out=g.psT[:, hh, s_ * 128:(s_ + 1) * 128], in_=otile[:, s_, cb * 128:(cb + 1) * 128], identity=g.ident_bf[:, :]),
                                     reads=[t_ot, g.t_const], writes=[g.t_psT])
                        P.act(lambda e, pr=pr: e.copy(out=oT[:, pr * 2:pr * 2 + 2, :], in_=g.psT[:, :, :]), reads=[g.t_psT], writes=[t_oT])
                    for cb in range(4):
                        P.dma(lambda e, cb=cb: e.dma_start(out=S["oT"][512 + cb * 128:512 + (cb + 1) * 128, tsl], in_=oT[:, cb, :]), reads=[t_oT], writes=[g.t_s["oT"][i]])
                return pe_part

            BRS = os.environ.get("NSA_BR", "012")
            inited = set()

            def epilogue(br, h, accv, acctok, qb=qb, q3=q3, first=False, inited=inited, BRS=BRS, finish=finish):
                if str(br) not in BRS:
                    if br == 0:
                        P.dve(lambda e: e.tensor_scalar(out=rl[:, :], in0=accv[:, :, 64], scalar1=1e-30, scalar2=None, op0=ALU.add), reads=[acctok], writes=[t_rl])
                        P.dve(lambda e: e.reciprocal(out=rl[:, :], in_=rl[:, :]), reads=[t_rl], writes=[t_rl])
                    return
                first = h not in inited
                inited.add(h)
                last_br = 1 if "1" in BRS else (2 if "2" in BRS else 0)
                P.dve(lambda e: e.tensor_scalar(out=rl[:, :], in0=accv[:, :, 64], scalar1=1e-30, scalar2=None, op0=ALU.add), reads=[acctok], writes=[t_rl])
                P.dve(lambda e: e.reciprocal(out=rl[:, :], in_=rl[:, :]), reads=[t_rl], writes=[t_rl])
                P.dve(lambda e: e.tensor_tensor(out=wgt[:, :], in0=rl[:, :], in1=gat[q3][:, :, h * 3 + br], op=ALU.mult), reads=[t_rl, t_Gg[q3]], writes=[t_wgt])
                for s_ in range(4):
                    if first:
                        P.dve(lambda e, s_=s_: e.tensor_scalar(out=ocomb[:, s_, h * 64:(h + 1) * 64], in0=accv[:, s_, 0:64], scalar1=wgt[:, s_:s_ + 1], scalar2=None, op0=ALU.mult),
                              reads=[acctok, t_wgt], writes=[t_oc])
                    else:
                        P.dve(lambda e, s_=s_: e.scalar_tensor_tensor(out=ocomb[:, s_, h * 64:(h + 1) * 64], in0=accv[:, s_, 0:64], scalar=wgt[:, s_:s_ + 1], in1=ocomb[:, s_, h * 64:(h + 1) * 64], op0=ALU.mult, op1=ALU.add),
                              reads=[acctok, t_wgt, t_oc], writes=[t_oc])
                if br == last_br and h == 7 and br != 0:
                    return finish()
                return None

            def select(gg, qb=qb, q3=q3):
                P.dve(lambda e: e.tensor_tensor(out=score[:], in0=pslc[:, gg, :, :], in1=vmt[q3][:], op=ALU.mult), reads=[t_pslc[gg], t_G[q3]], writes=[t_score])
                P.dve(lambda e: e.tensor_tensor(out=score[:], in0=score[:], in1=adt[q3][:], op=ALU.add), reads=[t_score, t_G[q3]], writes=[t_score])
                pst, ptok = misc[gg]
                for s_ in range(4):
                    P.dve(lambda e, s_=s_: e.max(out=mx[:, 0:8], in_=score[:, s_, :]), reads=[t_score], writes=[t_mx])
                    P.dve(lambda e, s_=s_: e.match_replace(out=work[:], in_to_replace=mx[:, 0:8], in_values=score[:, s_, :], imm_value=-2.0), reads=[t_score, t_mx], writes=[t_work])
                    P.dve(lambda e: e.max(out=mx[:, 8:16], in_=work[:]), reads=[t_work], writes=[t_mx])
                    P.dve(lambda e, s_=s_: e.tensor_scalar(out=selm1[gg][:, s_, :], in0=score[:, s_, :], scalar1=mx[:, 15:16], scalar2=-1.0, op0=ALU.is_ge, op1=ALU.add),
                          reads=[t_score, t_mx], writes=[t_selm1[gg]])

                def pe_part():
                    for s_ in range(4):
                        P.pe(lambda e, s_=s_, pst=pst: e.transpose(out=pst[0:NBS, s_ * 128:(s_ + 1) * 128], in_=selm1[gg][:, s_, :], identity=g.ident_f[:, :]),
                             reads=[t_selm1[gg], g.t_const], writes=[ptok])
                    P.act(lambda e, pst=pst: e.copy(out=selT[gg][0:NBS, :], in_=pst[0:NBS, :]), reads=[ptok], writes=[t_selT[gg]])
                return pe_part

            nchs = [n_ for n_ in range(NCH) if 2048 * n_ + 31 <= 512 * i + 511]
            for h in range(8):
                gg, j = h // 4, h % 4
                r0 = 64 * gg
                acc, acctok = accs[hcount % 2]
                hcount += 1
                accv = acc[:, :].rearrange("p (s c) -> p s c", c=128)
                pvs = mk_pv([(n_, 0, 4) for n_ in nchs], accv, WC, lambda n_, gg=gg: (Vc[:, n_, gg, 0:WC], t_Vc))
                for idx, n_ in enumerate(nchs):
                    b = Blk()
                    b.smm = [(KcT[r0:r0 + 64, n_ * 128:(n_ + 1) * 128], Q[qb][r0:r0 + 64, j, :], [t_Kc, t_Q[qb]]),
                             (g.ident_bf[:, :], cmpb[qb][:, n_, :], [g.t_const, t_Q[qb]])]
                    b.qlo, b.qhi, b.scale = 0, 512, scale
                    b.pv = pvs[idx]
                    b.acctok = acctok
                    b.post = None
                    if idx == len(nchs) - 1:
                        def post(h=h, gg=gg, accv=accv, acctok=acctok, epilogue=epilogue, select=select, finish=finish):
                            ret = []
                            epilogue(0, h, accv, acctok, first=True)
                            if h % 4 == 0:
                                P.dve(lambda e: e.memset(pslc[:, gg, :, :], 0.0), writes=[t_pslc[gg]])
                            for s_ in range(4):
                                P.dve(lambda e, s_=s_: e.scalar_tensor_tensor(out=pslc[:, gg, s_, 0:NR], in0=accv[:, s_, 65:65 + NR], scalar=rl[:, s_:s_ + 1], in1=pslc[:, gg, s_, 0:NR], op0=ALU.mult, op1=ALU.add),
                                      reads=[acctok, t_rl, t_pslc[gg]], writes=[t_pslc[gg]])
                            if h % 4 == 3:
                                ret.append(select(gg))
                            if h == 7 and BRS == "0":
                                ret.append(finish())
                            return ret
                        b.post = post
                    blks.append(b)
            wspecs = win_chunks(i)
            for h in (range(8) if "2" in BRS else []):
                gg, j = h // 4, h % 4
                r0 = 64 * gg
                acc, acctok = accs[hcount % 2]
                hcount += 1
                accv = acc[:, :].rearrange("p (s c) -> p s c", c=128)
                pvs = mk_pv([(c, slo, shi) for (c, kd, o, qlo, qhi, slo, shi) in wspecs], accv, 65, lambda c, gg=gg: (Vs[:, c, (2 + gg) * 65:(3 + gg) * 65], t_V))
                for idx, (c, kd, o, qlo, qhi, slo, shi) in enumerate(wspecs):
                    b = Blk()
                    mb = g.cbias if kd == "c" else g.lbias
                    b.smm = [(Kwin[r0:r0 + 64, c * 128:(c + 1) * 128], Q[qb][r0:r0 + 64, j, qlo:qhi], [t_K, t_Q[qb]]),
                             (g.ident_bf[:, :], mb[:, o, qlo:qhi], [g.t_const])]
                    b.qlo, b.qhi, b.scale = qlo, qhi, scale
                    b.pv = pvs[idx]
                    b.acctok = acctok
                    b.post = None
                    if idx == len(wspecs) - 1:
                        b.post = (lambda h=h, accv=accv, acctok=acctok, epilogue=epilogue: epilogue(2, h, accv, acctok))
                    blks.append(b)
            specs = causal_chunks(i)
            for h in (range(8) if "1" in BRS else []):
                gg, j = h // 4, h % 4
                r0 = 64 * gg
                acc, acctok = accs[hcount % 2]
                hcount += 1
                accv = acc[:, :].rearrange("p (s c) -> p s c", c=128)
                pvs = mk_pv([(c, slo, shi) for (c, kd, o, qlo, qhi, slo, shi) in specs], accv, 65, lambda c, gg=gg: (Vs[:, c, gg * 65:(gg + 1) * 65], t_V))
                for idx, (c, kd, o, qlo, qhi, slo, shi) in enumerate(specs):
                    b = Blk()
                    b.smm = [(Kslc[r0:r0 + 64, c * 128:(c + 1) * 128], Q[qb][r0:r0 + 64, j, qlo:qhi], [t_K, t_Q[qb]]),
                             (ebig[:, c * 128:(c + 1) * 128], selT[gg][:, qlo:qhi], [t_c, t_selT[gg]])]
                    if kd == "c":
                        b.smm.append((g.ident_bf[:, :], g.cbias[:, o, qlo:qhi], [g.t_const]))
                    b.qlo, b.qhi, b.scale = qlo, qhi, scale
                    b.pv = pvs[idx]
                    b.acctok = acctok
                    b.post = None
                    if idx == len(specs) - 1:
                        def post(h=h, accv=accv, acctok=acctok, tsl=tsl, i=i, epilogue=epilogue):
                            return epilogue(1, h, accv, acctok)
                        b.post = post
                    blks.append(b)
        run_blocks_ld(P, g, blks, sbanks, pts, t_pts, k2ld)


def wout_pass(P, nc, g, T, l, W):
    P.fence()
    S = g.S
    with ExitStack() as s:
        sb = lambda name, shape, dt: s.enter_context(nc.sbuf_tensor(g.nm(name), shape, dt))
        wo = sb("wo", [128, 8, D], BF16)
        xt = [sb("xt", [128, 8, TT], F32) for _ in range(2)]
        ot = [sb("ot", [128, 8, TT], BF16) for _ in range(2)]
        t_w = Tok()
        t_x = [Tok(), Tok()]
        t_o = [Tok(), Tok()]
        for c in range(8):
            P.dma(lambda e, c=c: e.dma_start(out=wo[:, c, :], in_=W["w_out"][l, c * 128:(c + 1) * 128, :]), writes=[t_w], q="pool")
        xs_v = g.xs.rearrange("(c p) t -> p c t", p=128)
        os_v = S["oT"].rearrange("(c p) t -> p c t", p=128)
        for it in range(T // TT):
            b_ = it % 2
            tsl = slice(it * TT, (it + 1) * TT)
            P.dma(lambda e, tsl=tsl, b_=b_: e.dma_start(out=xt[b_][:], in_=xs_v[:, :, tsl]), reads=[g.t_x[it]], writes=[t_x[b_]])
            P.dma(lambda e, tsl=tsl, b_=b_: e.dma_start(out=ot[b_][:], in_=os_v[:, :, tsl]), reads=[g.t_s["oT"][it]], writes=[t_o[b_]])
            for c in range(8):
                pst, ptok = nextps(g)
                for k in range(8):
                    M(P, pst[:, :], wo[:, k, c * 128:(c + 1) * 128], ot[b_][:, k, :], k == 0, k == 7, [t_w, t_o[b_]], [ptok])
                P.dve(lambda e, c=c, pst=pst, b_=b_: e.tensor_tensor(out=xt[b_][:, c, :], in0=pst[:, :], in1=xt[b_][:, c, :], op=ALU.add),
                      reads=[ptok, t_x[b_]], writes=[t_x[b_]])
            P.dma(lambda e, tsl=tsl, b_=b_: e.dma_start(out=xs_v[:, :, tsl], in_=xt[b_][:]), reads=[t_x[b_]], writes=[g.t_x[it]])


def w_specs(T):
    L = L_DEPTH
    NBS, NCH = T // 64, max(1, T // 2048)
    NR = NBS - 1
    f, b = "f32", "bf16"
    return {
        "ffn_g": ([L, 2, 128, 8], f), "mix_g": ([L, 128, 8], f), "fin_g": ([128, 8], f),
        "ffn_wg": ([L, 2, D, DFF], f), "ffn_wu": ([L, 2, D, DFF], f), "ffn_wd": ([L, 2, DFF, D], f),
        "w_inA": ([L, D, NWA], f), "w_uq": ([L, 256, 384], f), "w_ukvA": ([L, 128, 640], f),
        "qn_g": ([L, 128, 2], f), "kvn_g": ([L, 128, 1], f), "gate_b": ([L, 128, 24], f),
        "dlam": ([L, 128, 128], f), "dng": ([L, 128, 64], f),
        "cpos": ([L, 2, 128, 32], f), "cw1": ([L, 2, 128, 8192], f), "cb1": ([L, 2, 128, 2], f), "cw2": ([L, 2, 128, 256], f),
        "w_out": ([L, D, D], f),
        "rt_mla_c": ([96, T], f), "rt_mla_s": ([96, T], f), "rt_diff_c": ([96, T], f), "rt_diff_s": ([96, T], f),
        "rt_nsa_c": ([128, T], f), "rt_nsa_s": ([128, T], f),
        "w_inS": ([L, D, NSW], f), "w_uqS": ([L, 256, 384], f),
        "ident": ([128, 128], f), "cbias": ([128, 4, 512], b), "lbias": ([128, 4, 512], b),
        "cmpbias": ([NCH, 128, T], b), "ebig": ([128, T], b), "mapm": ([NCH * 128, NR], b),
        "validm": ([T, NBS], f), "addc": ([T, NBS], f),
    }


def s_specs(T):
    b = BF16
    return {
        "mla_qT": ([4, 96, T], b), "mla_kT": ([4, 96, T], b), "mla_v": ([T, 260], b),
        "diff_qT": ([3, 96, T], b), "diff_kT": ([3, 96, T], b),
        "nsa_qT": ([4, 128, T], b), "nsa_kT": ([3, 128, T], b), "nsa_vcT": ([128, T], b),
        "v_tm": ([T, 520], b), "gates": ([T, 24], F32), "oT": ([D, T], b),
    }


def build(T, stop_after=None, debug=False):
    nc = bass.Bass("TRN2", target_bir_lowering=False)
    dtm = {"f32": F32, "bf16": BF16}
    W = {k: nc.dram_tensor(k, shp, dtm[dt], kind="ExternalInput").ap() for k, (shp, dt) in w_specs(T).items()}
    xT = nc.dram_tensor("xT", [D, T], F32, kind="ExternalInput").ap()
    out = nc.dram_tensor("out", [D, T], F32, kind="ExternalOutput").ap()
    skind = "ExternalOutput" if debug else "Internal"
    g = G()
    g.S = {k: nc.dram_tensor("s_" + k, shp, dt, kind=skind).ap() for k, (shp, dt) in s_specs(T).items()}
    g.xs = nc.dram_tensor("s_xs", [D, T], F32, kind=skind).ap()
    NT = T // TT
    g.t_x = [Tok() for _ in range(NT)]
    g.t_s = {k: [Tok() for _ in range(NT)] for k in g.S}
    g.t_out = Tok()
    cnt = [0]

    def nm(base):
        cnt[0] += 1
        return f"sb{cnt[0]}_{base}"
    g.nm = nm
    g.pi = 0
    P = Prog(nc)
    with ExitStack() as st:
        g.ps = [(st.enter_context(nc.psum_tensor(f"ps{i}", [128, 512], F32)), Tok()) for i in range(7)]
        g.psT = st.enter_context(nc.psum_tensor("psT", [128, 2, 512], BF16))
        g.t_psT = Tok()
        g.ones_bf = st.enter_context(nc.sbuf_tensor("c_ones", [128, 128], BF16))
        g.ident_bf = st.enter_context(nc.sbuf_tensor("c_identb", [128, 128], BF16))
        g.ident_f = st.enter_context(nc.sbuf_tensor("c_identf", [128, 128], F32))
        g.eps = st.enter_context(nc.sbuf_tensor("c_eps", [128, 1], F32))
        g.cbias = st.enter_context(nc.sbuf_tensor("c_cbias", [128, 4, 512], BF16))
        g.lbias = st.enter_context(nc.sbuf_tensor("c_lbias", [128, 4, 512], BF16))
        g.t_const = Tok()
        g.zbias = st.enter_context(nc.sbuf_tensor("c_zbias", [128, 512], BF16))
        P.dve(lambda e: e.memset(g.zbias[:], 0.0), writes=[g.t_const])
        P.dve(lambda e: e.memset(g.ones_bf[:], 1.0), writes=[g.t_const])
        P.dve(lambda e: e.memset(g.eps[:], EPS), writes=[g.t_const])
        P.dma(lambda e: e.dma_start(out=g.ident_bf[:], in_=W["ident"]), writes=[g.t_const], q="pool")
        P.dma(lambda e: e.dma_start(out=g.ident_f[:], in_=W["ident"]), writes=[g.t_const])
        P.dma(lambda e: e.dma_start(out=g.cbias[:], in_=W["cbias"]), writes=[g.t_const])
        P.dma(lambda e: e.dma_start(out=g.lbias[:], in_=W["lbias"]), writes=[g.t_const])
        steps = []
        for l in range(L_DEPTH):
            lam_init = 0.8 - 0.6 * math.exp(-0.3 * l)
            last = (l == L_DEPTH - 1)
            steps.append(("ffn1", lambda l=l: ffn_pass(P, nc, g, T, xT if l == 0 else g.xs, g.xs, W["ffn_g"][l, 0], W["ffn_wg"][l, 0], W["ffn_wu"][l, 0], W["ffn_wd"][l, 0])))
            steps.append(("proj", lambda l=l: proj_pass(P, nc, g, T, l, W)))
            steps.append(("mla", lambda l=l: mla_pass(P, nc, g, T, l, W)))
            steps.append(("diff", lambda l=l, lam_init=lam_init: diff_pass(P, nc, g, T, l, W, lam_init)))
            steps.append(("nsa", lambda l=l: nsa_pass(P, nc, g, T, l, W)))
            steps.append(("wout", lambda l=l: wout_pass(P, nc, g, T, l, W)))
            if last:
                steps.append(("ffn2", lambda l=l: ffn_pass(P, nc, g, T, g.xs, g.xs, W["ffn_g"][l, 1], W["ffn_wg"][l, 1], W["ffn_wu"][l, 1], W["ffn_wd"][l, 1], fin_g=W["fin_g"], out_dst=out)))
            else:
                steps.append(("ffn2", lambda l=l: ffn_pass(P, nc, g, T, g.xs, g.xs, W["ffn_g"][l, 1], W["ffn_wg"][l, 1], W["ffn_wu"][l, 1], W["ffn_wd"][l, 1])))
        for si, (name, fn) in enumerate(steps):
            fn()
            g.pi = 0
            if stop_after is not None and si == stop_after:
                break
        nw = P.emit(st)
        g.nops, g.nw = len(P.ops), nw
        g.P = P
    return nc, g


def _rope_tab(T, dim):
    inv = (10000.0 ** (-np.arange(0, dim, 2, dtype=np.float32) / np.float32(dim))).astype(np.float32)
    ang = np.arange(T, dtype=np.float32)[:, None] * inv[None, :]
    return np.cos(ang).astype(np.float32).T, np.sin(ang).astype(np.float32).T


def host_consts(T):
    bf = ml_dtypes.bfloat16
    NBS, NCH = T // 64, max(1, T // 2048)
    NCMP = T // 16 - 1
    NR = NBS - 1
    C = {}
    c32, s32 = _rope_tab(T, 32)
    c64, s64 = _rope_tab(T, 64)
    mc = np.ones((96, T), np.float32); ms = np.zeros((96, T), np.float32)
    for r in range(32):
        mc[64 + r] = c32[r % 16]; ms[64 + r] = s32[r % 16] * (-1.0 if r < 16 else 1.0)
    dc = np.zeros((96, T), np.float32); ds = np.zeros((96, T), np.float32)
    for r in range(96):
        dc[r] = c32[(r % 32) % 16]; ds[r] = s32[(r % 32) % 16] * (-1.0 if (r % 32) < 16 else 1.0)
    ncn = np.zeros((128, T), np.float32); nsn = np.zeros((128, T), np.float32)
    for r in range(128):
        ncn[r] = c64[(r % 64) % 32]; nsn[r] = s64[(r % 64) % 32] * (-1.0 if (r % 64) < 32 else 1.0)
    C.update(rt_mla_c=mc, rt_mla_s=ms, rt_diff_c=dc, rt_diff_s=ds, rt_nsa_c=ncn, rt_nsa_s=nsn)

    C["ident"] = np.eye(128, dtype=np.float32)
    kk = np.arange(128)[:, None]
    qq = np.arange(512)[None, :]
    cb = np.zeros((128, 4, 512), np.float32); lb = np.zeros((128, 4, 512), np.float32)
    for o in range(4):
        cb[:, o, :] = np.where(128 * o + kk <= qq, 0.0, NEGB)
        lb[:, o, :] = np.where(qq < 128 * o + kk, 0.0, NEGB)
    C["cbias"] = cb.astype(bf); C["lbias"] = lb.astype(bf)
    n = np.arange(NCH * 128)[:, None]
    t = np.arange(T)[None, :]
    cmpb = np.where((16 * n + 31 <= t) & (n < NCMP), 0.0, NEGB).astype(np.float32)
    C["cmpbias"] = cmpb.reshape(NCH, 128, T).astype(bf)
    j = np.arange(NBS)[:, None]
    eb = np.zeros((128, T), np.float32)
    eb[:NBS] = np.where((t // 64) == j, -NEGB, 0.0)
    C["ebig"] = eb.astype(bf)
    mp = np.zeros((NCH * 128, NBS), np.float32)
    for jj in range(NBS):
        for m_ in range(4):
            for n_ in range(2):
                idx = 4 * jj - m_ - n_
                if 0 <= idx < NCMP:
                    mp[idx, jj] += 1.0
    C["mapm"] = mp[:, :NR].astype(bf)
    tp = np.arange(T)[:, None]
    jb = np.arange(NBS)[None, :]
    cblk = tp // 64
    valid = jb <= cblk
    forced = (jb == 0) | (jb == cblk) | (jb == cblk - 1)
    C["validm"] = valid.astype(np.float32)
    C["addc"] = np.where(valid, np.where(forced, 1e4, 0.0), -1.0).astype(np.float32)
    return C


def host_weights(inp):
    L = L_DEPTH
    f = np.float32
    Wd = {}
    pc = lambda v: np.ascontiguousarray(v.reshape(-1, 128).T)
    Wd["ffn_g"] = np.stack([np.stack([pc(inp["ffn_norm_g"][l, i]) for i in range(2)]) for l in range(L)]).astype(f)
    Wd["mix_g"] = np.stack([pc(inp["mix_norm_g"][l]) for l in range(L)]).astype(f)
    Wd["fin_g"] = pc(inp["final_norm_g"]).astype(f)
    Wd["ffn_wg"] = np.ascontiguousarray(inp["ffn_w_gate"], dtype=f)
    Wd["ffn_wu"] = np.ascontiguousarray(inp["ffn_w_up"], dtype=f)
    Wd["ffn_wd"] = np.ascontiguousarray(inp["ffn_w_down"], dtype=f)
    wA = np.zeros((L, D, NWA), f)
    wukvA = np.zeros((L, 128, 640), f)
    for l in range(L):
        w = np.asarray(inp["w_in"][l], dtype=f)
        cq, ckv, kr, dq, dk, dv, nq, nkv, ng = np.split(w, np.cumsum([256, 128, 32, 256, 256, 256, 512, 768])[:8].tolist(), axis=1)
        wA[l, :, O_CQ:O_CQ + 256] = cq
        wA[l, :, O_CKV:O_CKV + 128] = ckv
        wA[l, :, O_KR + 64:O_KR + 96] = kr
        for p in range(8):
            jj, base = p // 3, 32 * (p % 3)
            wA[l, :, O_DQ + jj * 96 + base:O_DQ + jj * 96 + base + 32] = dq[:, p * 32:(p + 1) * 32]
            wA[l, :, O_DK + jj * 96 + base:O_DK + jj * 96 + base + 32] = dk[:, p * 32:(p + 1) * 32]
        wA[l, :, O_DV:O_DV + 256] = dv
        for jj in range(4):
            wA[l, :, O_NQ + jj * 128:O_NQ + jj * 128 + 64] = nq[:, jj * 64:(jj + 1) * 64]
            wA[l, :, O_NQ + jj * 128 + 64:O_NQ + (jj + 1) * 128] = nq[:, (jj + 4) * 64:(jj + 5) * 64]
        for br in range(3):
            wA[l, :, O_NK + br * 128:O_NK + (br + 1) * 128] = nkv[:, br * 256:br * 256 + 128]
        wA[l, :, O_NVC:O_NVC + 128] = nkv[:, 128:256]
        wA[l, :, O_NVS:O_NVS + 128] = nkv[:, 256 + 128:256 + 256]
        wA[l, :, O_NVS + 128:O_NVS + 256] = nkv[:, 512 + 128:512 + 256]
        wA[l, :, O_NG:O_NG + 24] = ng
        wk = np.asarray(inp["mla_w_ukv"][l], dtype=f)
        for h in range(4):
            wukvA[l, :, h * 96:h * 96 + 64] = wk[:, h * 128:h * 128 + 64]
            wukvA[l, :, 384 + h * 64:384 + (h + 1) * 64] = wk[:, h * 128 + 64:(h + 1) * 128]
    Wd["w_inA"] = wA

    def swp(a, blk):
        n = a.shape[-1] // blk
        b_ = a.reshape(a.shape[:-1] + (n, 2, blk // 2))
        return np.ascontiguousarray(b_[..., ::-1, :].reshape(a.shape))
    wS = np.zeros((L, D, NSW), f)
    kr_pad = wA[:, :, O_KR:O_KR + 96].copy()
    kr_pad[:, :, 64:96] = swp(kr_pad[:, :, 64:96], 32)
    wS[:, :, SW_KR:SW_KR + 96] = kr_pad
    wS[:, :, SW_DQ:SW_DQ + 288] = swp(wA[:, :, O_DQ:O_DQ + 288], 32)
    wS[:, :, SW_DK:SW_DK + 288] = swp(wA[:, :, O_DK:O_DK + 288], 32)
    wS[:, :, SW_NQ:SW_NQ + 512] = swp(wA[:, :, O_NQ:O_NQ + 512], 64)
    wS[:, :, SW_NK:SW_NK + 384] = swp(wA[:, :, O_NK:O_NK + 384], 64)
    Wd["w_inS"] = wS
    wq = np.asarray(inp["mla_w_uq"], dtype=f).reshape(L, 256, 4, 96)
    wqS = np.zeros_like(wq)
    wqS[..., 64:96] = swp(wq[..., 64:96], 32)
    Wd["w_uqS"] = np.ascontiguousarray(wqS.reshape(L, 256, 384))
    Wd["w_uq"] = np.ascontiguousarray(inp["mla_w_uq"], dtype=f)
    Wd["w_ukvA"] = wukvA
    Wd["qn_g"] = np.stack([pc(inp["mla_q_norm_g"][l]) for l in range(L)]).astype(f)
    Wd["kvn_g"] = np.stack([pc(inp["mla_kv_norm_g"][l]) for l in range(L)]).astype(f)
    rep = lambda v: np.ascontiguousarray(np.broadcast_to(np.asarray(v, dtype=f).reshape(1, -1), (128, np.asarray(v).size)))
    Wd["gate_b"] = np.stack([rep(inp["nsa_gate_b"][l]) for l in range(L)])
    Wd["dlam"] = np.stack([rep(inp["diff_lambda"][l]) for l in range(L)])
    Wd["dng"] = np.stack([rep(inp["diff_norm_g"][l]) for l in range(L)])
    cpos = np.zeros((L, 2, 128, 32), f); cw1 = np.zeros((L, 2, 128, 8192), f)
    cb1 = np.zeros((L, 2, 128, 2), f); cw2 = np.zeros((L, 2, 128, 256), f)
    for l in range(L):
        for kv in range(2):
            pT = np.asarray(inp["nsa_cmp_pos"][l, kv], dtype=f).T
            cpos[l, kv, 0:64] = pT; cpos[l, kv, 64:128] = pT
            w1 = np.asarray(inp["nsa_cmp_w1"][l, kv], dtype=f).reshape(32, 64, 256).transpose(1, 0, 2).reshape(64, 8192)
            cw1[l, kv, 0:64] = w1; cw1[l, kv, 64:128] = w1
            cb1[l, kv] = pc(np.asarray(inp["nsa_cmp_b1"][l, kv], dtype=f))
            w2 = np.asarray(inp["nsa_cmp_w2"][l, kv], dtype=f).reshape(2, 128, 64).transpose(1, 0, 2)
            tmp = np.zeros((128, 2, 128), f); tmp[:, :, 64:128] = w2
            cw2[l, kv] = tmp.reshape(128, 256)
    Wd.update(cpos=cpos, cw1=cw1, cb1=cb1, cw2=cw2)
    Wd["w_out"] = np.ascontiguousarray(inp["w_out"], dtype=f)
    return Wd


_CACHE = {}


def kernel(**inputs):
    x = np.asarray(inputs["x"], dtype=np.float32)
    B, T, _ = x.shape
    if T not in _CACHE:
        _CACHE[T] = build(T)[0]
    nc = _CACHE[T]
    Wd = host_weights(inputs)
    Wd.update(host_consts(T))
    in_maps = []
    for b in range(B):
        m = dict(Wd)
        m["xT"] = np.ascontiguousarray(x[b].T)
        in_maps.append(m)
    res = run_bass_kernel_spmd(nc, in_maps, core_ids=list(range(B)))
    out = np.stack([np.asarray(r["out"]).T for r in res.results]).astype(np.float32)
    return out
```

```python
import numpy as np
import concourse.bass as bass
import concourse.mybir as mybir

F32 = mybir.dt.float32
BF16 = mybir.dt.bfloat16
AF = mybir.ActivationFunctionType
ALU = mybir.AluOpType
AX = mybir.AxisListType

NDMASEM = 8


class Tok:
    __slots__ = ("w", "r", "name", "wpar")

    def __init__(self, name=""):
        self.w = []
        self.r = []
        self.name = name
        self.wpar = False


class Op:
    __slots__ = ("eng", "fn", "deps", "idx", "dma", "sig", "rawdeps", "dsem", "dval", "fdeps")

    def __init__(self, eng, fn, idx, dma):
        self.eng = eng
        self.fn = fn
        self.idx = idx
        self.dma = dma
        self.deps = set()
        self.rawdeps = set()
        self.sig = None
        self.dsem = None
        self.dval = None
        self.fdeps = ()


class Prog:
    ENGS = ("pe", "act", "dve", "pool", "sp")

    def __init__(self, nc):
        self.nc = nc
        self.ops = []
        self.dma_count = {e: 0 for e in self.ENGS}
        self.dma_hist = {e: [] for e in self.ENGS}
        self.last = {e: None for e in self.ENGS}
        self.fence_deps = {e: set() for e in self.ENGS}

    def fence(self):
        deps = set()
        for e in self.ENGS:
            if self.last[e] is not None:
                deps.add(self.last[e])
            deps.update(self.dma_hist[e][-NDMASEM:])
        for e in self.ENGS:
            self.fence_deps[e] = set(deps)

    def eng_handle(self, e):
        nc = self.nc
        return {"pe": nc.tensor, "act": nc.scalar, "dve": nc.vector,
                "pool": nc.gpsimd, "sp": nc.sync}[e]

    def add(self, eng, fn, reads=(), writes=(), dma=False, par=False):
        idx = len(self.ops)
        op = Op(eng, fn, idx, dma)
        for t in reads:
            for w in t.w:
                op.deps.add(w)
                op.rawdeps.add(w)
        for t in writes:
            if not (par and t.wpar):
                for w in t.w:
                    op.deps.add(w)
            for r in t.r:
                op.deps.add(r)
        for t in reads:
            t.r.append(idx)
        for t in writes:
            if par and t.wpar:
                t.w.append(idx)
            else:
                t.w = [idx]
                t.wpar = par
                t.r = []
        if dma:
            j = self.dma_count[eng]
            self.dma_count[eng] = j + 1
            op.dsem = j % NDMASEM
            op.dval = 16 * (j // NDMASEM + 1)
            hist = self.dma_hist[eng]
            if j >= NDMASEM:
                op.deps.add(hist[j - NDMASEM])
            hist.append(idx)
        if self.fence_deps[eng]:
            op.deps.update(self.fence_deps[eng])
            op.fdeps = set(self.fence_deps[eng])
            self.fence_deps[eng] = set()
        if not dma:
            self.last[eng] = idx
        op.deps.discard(idx)
        self.ops.append(op)
        return op

    def pe(self, fn, reads=(), writes=()):
        return self.add("pe", fn, reads, writes)

    def act(self, fn, reads=(), writes=()):
        return self.add("act", fn, reads, writes)

    def dve(self, fn, reads=(), writes=()):
        return self.add("dve", fn, reads, writes)

    def pool(self, fn, reads=(), writes=()):
        return self.add("pool", fn, reads, writes)

    def dma(self, fn, reads=(), writes=(), q="sp", par=True):
        return self.add(q, fn, reads, writes, dma=True, par=par)

    def emit(self, stack):
        nc = self.nc
        ops = self.ops
        need = []
        for op in ops:
            nd = []
            for d in op.deps:
                dop = ops[d]
                if dop.dma or op.dma:
                    nd.append(d)
                elif dop.eng == op.eng:
                    if op.eng != "pe" or d in op.fdeps:
                        nd.append(d)
                else:
                    nd.append(d)
            need.append(nd)
        seen_idx = {e: {} for e in self.ENGS}
        pruned = []
        for op in ops:
            best = {}
            keep = []
            for d in need[op.idx]:
                dop = ops[d]
                if dop.dma:
                    keep.append(d)
                else:
                    if d > best.get(dop.eng, -1):
                        best[dop.eng] = d
            si = seen_idx[op.eng]
            for e_, d in best.items():
                if si.get(e_, -1) >= d:
                    continue
                si[e_] = d
                keep.append(d)
            pruned.append(keep)
        need = pruned
        signaled = set()
        for nd in need:
            for d in nd:
                if not ops[d].dma:
                    signaled.add(d)
        cnt = {e: 0 for e in self.ENGS}
        for op in ops:
            if op.idx in signaled:
                cnt[op.eng] += 1
                op.sig = cnt[op.eng]
        csem = {e: stack.enter_context(nc.semaphore(f"c_{e}")) for e in self.ENGS if e != "sp"}
        dsem = {}
        for e in self.ENGS:
            if self.dma_count[e] > 0:
                dsem[e] = [stack.enter_context(nc.semaphore(f"d_{e}_{i}")) for i in range(NDMASEM)]
        seen = {e: {} for e in self.ENGS}
        nwait = 0
        self.dump = None
        for op in ops:
            h = self.eng_handle(op.eng)
            sn = seen[op.eng]
            w = {}
            for d in need[op.idx]:
                dop = ops[d]
                if dop.dma:
                    key = ("d", dop.eng, dop.dsem)
                    val = dop.dval
                else:
                    key = ("c", dop.eng)
                    val = dop.sig
                if val > w.get(key, 0):
                    w[key] = val
            for key, val in w.items():
                if sn.get(key, 0) >= val:
                    continue
                sn[key] = val
                sem = csem[key[1]] if key[0] == "c" else dsem[key[1]][key[2]]
                h.wait_ge(sem, val)
                nwait += 1
            if self.dump is not None:
                self.dump.append((op.idx, op.eng, op.dma, dict(w), op.sig, (op.dsem, op.dval) if op.dma else None))
            ins = op.fn(h)
            if op.dma:
                ins.then_inc(dsem[op.eng][op.dsem], 16)
            elif op.sig is not None:
                ins.then_inc(csem[op.eng], 1)
        for e in self.ENGS:
            n = self.dma_count[e]
            if n == 0:
                continue
            h = self.eng_handle(e)
            for i in range(NDMASEM):
                k = (n - i + NDMASEM - 1) // NDMASEM
                if k > 0:
                    h.wait_ge(dsem[e][i], 16 * k)
        self.nwait = nwait
        self.cnt = cnt
        return nwait


from contextlib import ExitStack
import math
import ml_dtypes
from concourse.bass_utils import run_bass_kernel_spmd

D = 1024
DFF = 2816
NF = DFF // 128
EPS = 1e-6
TT = 512
L_DEPTH = 2
NEGB = -30720.0
NWA = 2624
O_CQ, O_CKV, O_KR, O_DQ, O_DK, O_DV, O_NQ, O_NK, O_NVC, O_NVS, O_NG = (
    0, 256, 384, 480, 768, 1056, 1312, 1824, 2208, 2336, 2592)


NSW = 1568
SW_KR, SW_DQ, SW_DK, SW_NQ, SW_NK = 0, 96, 384, 672, 1184


class G:
    pass


def M(P, out, lhsT, rhs, start, stop, rd, wr):
    P.pe(lambda e: e.matmul(out, lhsT=lhsT, rhs=rhs, start=start, stop=stop), rd, wr)


def nextps(g):
    r = g.ps[g.pi % len(g.ps)]
    g.pi += 1
    return r


def rms_rstd(P, g, chunks, sq, t_sq, nfeat, rstd, t_rstd, rows=128):
    n = len(chunks)
    for c, (ap, tk) in enumerate(chunks):
        P.pool(lambda e, c=c, ap=ap: e.tensor_tensor(out=sq[0:rows, c, :], in0=ap, in1=ap, op=ALU.mult),
               reads=[tk], writes=[t_sq[c]])
    pst, pt = nextps(g)
    for c in range(n):
        M(P, pst[:, :], g.ones_bf[0:rows, :], sq[0:rows, c, :], c == 0, c == n - 1, [t_sq[c], g.t_const], [pt])
    P.act(lambda e: e.activation(out=rstd, in_=pst[:, :], func=AF.Sqrt, bias=g.eps[:, 0:1], scale=1.0 / nfeat),
          reads=[pt, g.t_const], writes=[t_rstd])
    P.dve(lambda e: e.reciprocal(out=rstd, in_=rstd), reads=[t_rstd], writes=[t_rstd])


def ffn_pass(P, nc, g, T, x_src, x_dst, g_dram, wg_d, wu_d, wd_d, fin_g=None, out_dst=None):
    P.fence()
    with ExitStack() as s:
        wg = s.enter_context(nc.sbuf_tensor(g.nm("wg"), [128, 8, DFF], BF16))
        wu = s.enter_context(nc.sbuf_tensor(g.nm("wu"), [128, 8, DFF], BF16))
        wd = s.enter_context(nc.sbuf_tensor(g.nm("wd"), [128, NF, D], BF16))
        xt = s.enter_context(nc.sbuf_tensor(g.nm("xt"), [128, 8, TT], F32))
        ht = s.enter_context(nc.sbuf_tensor(g.nm("ht"), [128, 8, TT], BF16))
        at = s.enter_context(nc.sbuf_tensor(g.nm("at"), [128, NF, TT], BF16))
        rstd = s.enter_context(nc.sbuf_tensor(g.nm("rstd"), [128, TT], F32))
        sg = [s.enter_context(nc.sbuf_tensor(g.nm("sg"), [128, TT], F32)) for i in range(2)]
        gt = s.enter_context(nc.sbuf_tensor(g.nm("gt"), [128, 16], F32))
        t_wg = [Tok() for _ in range(8)]
        t_wu = [Tok() for _ in range(8)]
        t_wd = [Tok() for _ in range(NF)]
        t_x, t_h, t_rstd, t_g = Tok(), Tok(), Tok(), Tok()
        t_a = [Tok() for _ in range(NF)]
        t_sg = [Tok(), Tok()]
        P.dma(lambda e: e.dma_start(out=gt[:, 0:8], in_=g_dram), writes=[t_g])
        if fin_g is not None:
            P.dma(lambda e: e.dma_start(out=gt[:, 8:16], in_=fin_g), writes=[t_g])
        for c in range(8):
            P.dma(lambda e, c=c: e.dma_start(out=wg[:, c, :], in_=wg_d[c * 128:(c + 1) * 128, :]),
                  writes=[t_wg[c]], q="pool")
            P.dma(lambda e, c=c: e.dma_start(out=wu[:, c, :], in_=wu_d[c * 128:(c + 1) * 128, :]),
                  writes=[t_wu[c]], q="pool")
        for f in range(NF):
            P.dma(lambda e, f=f: e.dma_start(out=wd[:, f, :], in_=wd_d[f * 128:(f + 1) * 128, :]),
                  writes=[t_wd[f]], q="pool")
        xs_v = x_src.rearrange("(c p) t -> p c t", p=128)
        xd_v = x_dst.rearrange("(c p) t -> p c t", p=128)
        for it in range(T // TT):
            tsl = slice(it * TT, (it + 1) * TT)
            P.dma(lambda e, tsl=tsl: e.dma_start(out=xt[:], in_=xs_v[:, :, tsl]),
                  reads=[g.t_x[it]], writes=[t_x])
            rms_rstd(P, g, [(xt[:, c, :], t_x) for c in range(8)], at, t_a, D, rstd[:], t_rstd)
            for c in range(8):
                P.dve(lambda e, c=c: e.scalar_tensor_tensor(out=ht[:, c, :], in0=xt[:, c, :], scalar=gt[:, c:c + 1], in1=rstd[:], op0=ALU.mult, op1=ALU.mult),
                      reads=[t_x, t_rstd, t_g], writes=[t_h])
            for f in range(NF):
                pg_, tg_ = nextps(g)
                pu_, tu_ = nextps(g)
                fs = slice(f * 128, (f + 1) * 128)
                for c in range(8):
                    M(P, pg_[:, :], wg[:, c, fs], ht[:, c, :], c == 0, c == 7, [t_wg[c], t_h], [tg_])
                for c in range(8):
                    M(P, pu_[:, :], wu[:, c, fs], ht[:, c, :], c == 0, c == 7, [t_wu[c], t_h], [tu_])
                sgi = f % 2
                P.act(lambda e, pg_=pg_, sgi=sgi: e.activation(out=sg[sgi][:], in_=pg_[:, :], func=AF.Silu),
                      reads=[tg_], writes=[t_sg[sgi]])
                P.dve(lambda e, pu_=pu_, sgi=sgi, f=f: e.tensor_tensor(out=at[:, f, :], in0=sg[sgi][:], in1=pu_[:, :], op=ALU.mult),
                      reads=[tu_, t_sg[sgi]], writes=[t_a[f]])
            for c in range(8):
                py_, ty_ = nextps(g)
                cs = slice(c * 128, (c + 1) * 128)
                for f in range(NF):
                    M(P, py_[:, :], wd[:, f, cs], at[:, f, :], f == 0, f == NF - 1, [t_wd[f], t_a[f]], [ty_])
                P.dve(lambda e, c=c, py_=py_: e.scalar_tensor_tensor(out=xt[:, c, :], in0=py_[:, :], scalar=0.5, in1=xt[:, c, :], op0=ALU.mult, op1=ALU.add),
                      reads=[ty_, t_x], writes=[t_x])
            if fin_g is None:
                P.dma(lambda e, tsl=tsl: e.dma_start(out=xd_v[:, :, tsl], in_=xt[:]), reads=[t_x], writes=[g.t_x[it]])
            else:
                rms_rstd(P, g, [(xt[:, c, :], t_x) for c in range(8)], at, t_a, D, rstd[:], t_rstd)
                for c in range(8):
                    P.dve(lambda e, c=c: e.scalar_tensor_tensor(out=xt[:, c, :], in0=xt[:, c, :], scalar=gt[:, 8 + c:9 + c], in1=rstd[:], op0=ALU.mult, op1=ALU.mult),
                          reads=[t_x, t_rstd, t_g], writes=[t_x])
                od_v = out_dst.rearrange("(c p) t -> p c t", p=128)
                P.dma(lambda e, tsl=tsl: e.dma_start(out=od_v[:, :, tsl], in_=xt[:]), reads=[t_x], writes=[g.t_out])


def proj_pass(P, nc, g, T, l, W):
    P.fence()
    S = g.S
    with ExitStack() as s:
        sb = lambda name, shape, dt: s.enter_context(nc.sbuf_tensor(g.nm(name), shape, dt))
        win = sb("win", [128, 8, NWA], BF16)
        wuq = sb("wuq", [128, 2, 384], BF16)
        wukv = sb("wukv", [128, 640], BF16)
        wsw = sb("wsw", [128, 8, NSW], BF16)
        wuqs = sb("wuqs", [128, 2, 384], BF16)
        gt = sb("gt", [128, 8], F32)
        qng = sb("qng", [128, 2], F32)
        kvng = sb("kvng", [128, 1], F32)
        gateb = sb("gateb", [128, 24], F32)
        xt = sb("xt", [128, 8, TT], F32)
        ht = sb("ht", [128, 8, TT], BF16)
        sq = sb("sq", [128, 8, TT], BF16)
        rstd = sb("rstd", [128, TT], F32)
        rstd2 = sb("rstd2", [128, TT], F32)
        cqf = sb("cqf", [128, 2, TT], F32)
        cqn = sb("cqn", [128, 2, TT], BF16)
        ckvf = sb("ckvf", [128, 1, TT], F32)
        ckvn = sb("ckvn", [128, TT], BF16)
        tabs = {k: sb("tab_" + k, [128, TT], F32) for k in ("mc", "ms", "dc", "ds", "nc", "ns")}
        t1 = [sb("t1", [128, TT], F32) for _ in range(2)]
        t2 = [sb("t2", [128, TT], F32) for _ in range(2)]
        stg = sb("stg", [128, 22, TT], BF16)
        vstg = sb("vstg", [128, 4, 520], BF16)
        mvstg = sb("mvstg", [128, 4, 260], BF16)
        gstg = sb("gstg", [128, 4, 24], F32)
        t_w = Tok()
        t_x, t_h, t_rstd, t_rstd2, t_cqf, t_cqn, t_ckvf, t_ckvn, t_tab = (Tok() for _ in range(9))
        t_sq = [Tok() for _ in range(8)]
        t_xb = [Tok(), Tok()]
        t_t1 = [Tok(), Tok()]
        t_t2 = [Tok(), Tok()]
        t_stg = [Tok() for _ in range(22)]
        t_vstg, t_mvstg, t_gstg = Tok(), Tok(), Tok()
        for c in range(8):
            P.dma(lambda e, c=c: e.dma_start(out=win[:, c, :], in_=W["w_inA"][l, c * 128:(c + 1) * 128, :]), writes=[t_w], q="pool")
        for c in range(2):
            P.dma(lambda e, c=c: e.dma_start(out=wuq[:, c, :], in_=W["w_uq"][l, c * 128:(c + 1) * 128, :]), writes=[t_w], q="pool")
        P.dma(lambda e: e.dma_start(out=wukv[:], in_=W["w_ukvA"][l]), writes=[t_w], q="pool")
        for c in range(8):
            P.dma(lambda e, c=c: e.dma_start(out=wsw[:, c, :], in_=W["w_inS"][l, c * 128:(c + 1) * 128, :]), writes=[t_w], q="pool")
        for c in range(2):
            P.dma(lambda e, c=c: e.dma_start(out=wuqs[:, c, :], in_=W["w_uqS"][l, c * 128:(c + 1) * 128, :]), writes=[t_w], q="pool")
        P.dma(lambda e: e.dma_start(out=gt[:], in_=W["mix_g"][l]), writes=[t_w])
        P.dma(lambda e: e.dma_start(out=qng[:], in_=W["qn_g"][l]), writes=[t_w])
        P.dma(lambda e: e.dma_start(out=kvng[:], in_=W["kvn_g"][l]), writes=[t_w])
        P.dma(lambda e: e.dma_start(out=gateb[:], in_=W["gate_b"][l]), writes=[t_w])
        xs_v = g.xs.rearrange("(c p) t -> p c t", p=128)
        rot = [0]
        P.dve(lambda e: e.memset(vstg[:], 1.0), writes=[t_vstg])
        P.dve(lambda e: e.memset(mvstg[:], 1.0), writes=[t_mvstg])

        def rope(pst, ptok, p2, p2t, rows, ck, sk, dst, dtok):
            i = rot[0] % 2
            rot[0] += 1
            P.dve(lambda e: e.tensor_tensor(out=t1[i][0:rows, :], in0=pst[0:rows, :], in1=tabs[ck][0:rows, :], op=ALU.mult),
                  reads=[ptok, t_tab], writes=[t_t1[i]])
            P.dve(lambda e: e.tensor_tensor(out=t2[i][0:rows, :], in0=p2[0:rows, :], in1=tabs[sk][0:rows, :], op=ALU.mult),
                  reads=[p2t, t_tab], writes=[t_t2[i]])
            P.pool(lambda e: e.tensor_tensor(out=dst, in0=t1[i][0:rows, :], in1=t2[i][0:rows, :], op=ALU.add),
                   reads=[t_t1[i], t_t2[i]], writes=[dtok])

        for it in range(T // TT):
            tsl = slice(it * TT, (it + 1) * TT)
            P.dma(lambda e, tsl=tsl: e.dma_start(out=xt[:], in_=xs_v[:, :, tsl]), reads=[g.t_x[it]], writes=[t_x])
            for k, nm_, rows in (("mc", "rt_mla_c", 96), ("ms", "rt_mla_s", 96), ("dc", "rt_diff_c", 96),
                                 ("ds", "rt_diff_s", 96), ("nc", "rt_nsa_c", 128), ("ns", "rt_nsa_s", 128)):
                P.dma(lambda e, k=k, nm_=nm_, rows=rows, tsl=tsl: e.dma_start(out=tabs[k][0:rows, :], in_=W[nm_][:, tsl]), writes=[t_tab])
            rms_rstd(P, g, [(xt[:, c, :], t_x) for c in range(8)], sq, t_sq, D, rstd[:], t_rstd)
            for c in range(8):
                P.dve(lambda e, c=c: e.scalar_tensor_tensor(out=ht[:, c, :], in0=xt[:, c, :], scalar=gt[:, c:c + 1], in1=rstd[:], op0=ALU.mult, op1=ALU.mult),
                      reads=[t_x, t_rstd, t_w], writes=[t_h])

            def fm(col0, m, pst, ptok, start=True, stop=True, nck=8, wt=None):
                wt = win if wt is None else wt
                for c in range(nck):
                    M(P, pst[0:m, :], wt[:, c, col0:col0 + m], ht[:, c, :], start and c == 0, stop and c == nck - 1, [t_w, t_h], [ptok])

            import os
            STG = int(os.environ.get("PROJ_STAGE", "99"))
            if STG < 2:
                continue
            for j in range(2):
                pst, ptok = nextps(g)
                fm(O_CQ + j * 128, 128, pst, ptok)
                P.act(lambda e, j=j, pst=pst: e.copy(out=cqf[:, j, :], in_=pst[:, :]), reads=[ptok], writes=[t_cqf])
            rms_rstd(P, g, [(cqf[:, j, :], t_cqf) for j in range(2)], sq, t_sq, 256, rstd2[:], t_rstd2)
            for j in range(2):
                P.dve(lambda e, j=j: e.scalar_tensor_tensor(out=cqn[:, j, :], in0=cqf[:, j, :], scalar=qng[:, j:j + 1], in1=rstd2[:], op0=ALU.mult, op1=ALU.mult),
                      reads=[t_cqf, t_rstd2, t_w], writes=[t_cqn])
            pst, ptok = nextps(g)
            fm(O_CKV, 128, pst, ptok)
            P.act(lambda e, pst=pst: e.copy(out=ckvf[:, 0, :], in_=pst[:, :]), reads=[ptok], writes=[t_ckvf])
            rms_rstd(P, g, [(ckvf[:, 0, :], t_ckvf)], sq, t_sq, 128, rstd2[:], t_rstd2)
            P.dve(lambda e: e.scalar_tensor_tensor(out=ckvn[:], in0=ckvf[:, 0, :], scalar=kvng[:, 0:1], in1=rstd2[:], op0=ALU.mult, op1=ALU.mult),
                  reads=[t_ckvf, t_rstd2, t_w], writes=[t_ckvn])
            gi = 0
            if STG < 3:
                continue
            for h in range(4):
                pst, ptok = nextps(g)
                p2, p2t = nextps(g)
                for j in range(2):
                    M(P, pst[0:96, :], wuq[:, j, h * 96:(h + 1) * 96], cqn[:, j, :], j == 0, j == 1, [t_w, t_cqn], [ptok])
                for j in range(2):
                    M(P, p2[0:96, :], wuqs[:, j, h * 96:(h + 1) * 96], cqn[:, j, :], j == 0, j == 1, [t_w, t_cqn], [p2t])
                rope(pst, ptok, p2, p2t, 96, "mc", "ms", stg[0:96, gi, :], t_stg[gi])
                P.dma(lambda e, h=h, gi=gi, tsl=tsl: e.dma_start(out=S["mla_qT"][h, :, tsl], in_=stg[0:96, gi, :]), reads=[t_stg[gi]], writes=[g.t_s["mla_qT"][it]])
                gi += 1
            for h in range(4):
                pst, ptok = nextps(g)
                p2, p2t = nextps(g)
                M(P, pst[0:96, :], wukv[:, h * 96:(h + 1) * 96], ckvn[:], True, False, [t_w, t_ckvn], [ptok])
                fm(O_KR, 96, pst, ptok, start=False, stop=True)
                fm(SW_KR, 96, p2, p2t, wt=wsw)
                rope(pst, ptok, p2, p2t, 96, "mc", "ms", stg[0:96, gi, :], t_stg[gi])
                P.dma(lambda e, h=h, gi=gi, tsl=tsl: e.dma_start(out=S["mla_kT"][h, :, tsl], in_=stg[0:96, gi, :]), reads=[t_stg[gi]], writes=[g.t_s["mla_kT"][it]])
                gi += 1
            for nm_, off, offs in (("diff_qT", O_DQ, SW_DQ), ("diff_kT", O_DK, SW_DK)):
                for j in range(3):
                    pst, ptok = nextps(g)
                    p2, p2t = nextps(g)
                    fm(off + j * 96, 96, pst, ptok)
                    fm(offs + j * 96, 96, p2, p2t, wt=wsw)
                    rope(pst, ptok, p2, p2t, 96, "dc", "ds", stg[0:96, gi, :], t_stg[gi])
                    P.dma(lambda e, nm_=nm_, j=j, gi=gi, tsl=tsl: e.dma_start(out=S[nm_][j, :, tsl], in_=stg[0:96, gi, :]), reads=[t_stg[gi]], writes=[g.t_s[nm_][it]])
                    gi += 1
            if STG < 6:
                continue
            for nm_, off, offs, n in (("nsa_qT", O_NQ, SW_NQ, 4), ("nsa_kT", O_NK, SW_NK, 3)):
                for j in range(n):
                    pst, ptok = nextps(g)
                    p2, p2t = nextps(g)
                    fm(off + j * 128, 128, pst, ptok)
                    fm(offs + j * 128, 128, p2, p2t, wt=wsw)
                    rope(pst, ptok, p2, p2t, 128, "nc", "ns", stg[:, gi, :], t_stg[gi])
                    P.dma(lambda e, nm_=nm_, j=j, gi=gi, tsl=tsl: e.dma_start(out=S[nm_][j, :, tsl], in_=stg[:, gi, :]), reads=[t_stg[gi]], writes=[g.t_s[nm_][it]])
                    gi += 1
            if STG < 7:
                continue
            pst, ptok = nextps(g)
            fm(O_NVC, 128, pst, ptok)
            P.act(lambda e, pst=pst, gi=gi: e.copy(out=stg[:, gi, :], in_=pst[:, :]), reads=[ptok], writes=[t_stg[gi]])
            P.dma(lambda e, gi=gi, tsl=tsl: e.dma_start(out=S["nsa_vcT"][:, tsl], in_=stg[:, gi, :]), reads=[t_stg[gi]], writes=[g.t_s["nsa_vcT"][it]])
            gi += 1
            if STG < 8:
                continue
            TMS = os.environ.get("TM_SKIP", "")
            for s_ in range(4):
                ssl = slice(s_ * 128, (s_ + 1) * 128)
                if "v" not in TMS:
                    pst, ptok = nextps(g)
                    for c in range(8):
                        M(P, pst[:, 0:256], ht[:, c, ssl], win[:, c, O_DV:O_DV + 256], c == 0, c == 7, [t_w, t_h], [ptok])
                    for c in range(8):
                        M(P, pst[:, 256:512], ht[:, c, ssl], win[:, c, O_NVS:O_NVS + 256], c == 0, c == 7, [t_w, t_h], [ptok])
                    P.act(lambda e, pst=pst, s_=s_: e.copy(out=vstg[:, s_, :].rearrange("p (j d) -> p j d", d=65)[:, :, 0:64], in_=pst[:, :].rearrange("p (j d) -> p j d", d=64)),
                          reads=[ptok], writes=[t_vstg])
                if "m" not in TMS:
                    pst, ptok = nextps(g)
                    M(P, pst[:, 0:256], ckvn[:, ssl], wukv[:, 384:640], True, True, [t_w, t_ckvn], [ptok])
                    P.act(lambda e, pst=pst, s_=s_: e.copy(out=mvstg[:, s_, :].rearrange("p (j d) -> p j d", d=65)[:, :, 0:64], in_=pst[:, 0:256].rearrange("p (j d) -> p j d", d=64)),
                          reads=[ptok], writes=[t_mvstg])
                if "g" not in TMS:
                    pst, ptok = nextps(g)
                    for c in range(8):
                        M(P, pst[:, 0:24], ht[:, c, ssl], win[:, c, O_NG:O_NG + 24], c == 0, c == 7, [t_w, t_h], [ptok])
                    P.dve(lambda e, pst=pst, s_=s_: e.tensor_tensor(out=gstg[:, s_, :], in0=pst[:, 0:24], in1=gateb[:], op=ALU.add),
                          reads=[ptok, t_w], writes=[t_gstg])
            if "g" not in TMS:
                P.dma(lambda e, tsl=tsl: e.dma_start(out=S["gates"][tsl, :].rearrange("(s p) c -> p s c", p=128), in_=gstg[:]), reads=[t_gstg], writes=[g.t_s["gates"][it]])
            if "v" not in TMS:
                P.dma(lambda e, tsl=tsl: e.dma_start(out=S["v_tm"][tsl, :].rearrange("(s p) c -> p s c", p=128), in_=vstg[:]), reads=[t_vstg], writes=[g.t_s["v_tm"][it]])
            if "m" not in TMS:
                P.dma(lambda e, tsl=tsl: e.dma_start(out=S["mla_v"][tsl, :].rearrange("(s p) c -> p s c", p=128), in_=mvstg[:]), reads=[t_mvstg], writes=[g.t_s["mla_v"][it]])


class Blk:
    __slots__ = ("smm", "qlo", "qhi", "scale", "pv", "acctok", "post")


def mk_pv(chunk_specs, acc_view, width, v_fn):
    out = []
    n = len(chunk_specs)
    for idx, (c, slo, shi) in enumerate(chunk_specs):
        vap, vtok = v_fn(c)
        lst = []
        for s_ in range(slo, shi):
            st = (idx == 0 and s_ == slo)
            sp = (idx == n - 1 and s_ == shi - 1)
            lst.append((acc_view[:, s_, 0:width], s_, vap, vtok, st, sp))
        out.append(lst)
    return out


def causal_chunks(i):
    import os
    res = []
    for c in range(4 * i):
        if os.environ.get("DIAGONLY"):
            continue
        res.append((c, None, 0, 0, 512, 0, 4))
    for o in range(4):
        res.append((4 * i + o, "c", o, 128 * o, 512, o, 4))
    return res


def win_chunks(i):
    res = []
    for o in range(4):
        c = 4 * i - 4 + o
        if c >= 0:
            res.append((c, "l", o, 0, 128 * (o + 1), 0, o + 1))
    for o in range(4):
        res.append((4 * i + o, "c", o, 128 * o, 512, o, 4))
    return res


def mla_pass(P, nc, g, T, l, W):
    P.fence()
    S = g.S
    NC_ = T // 128
    NQB = T // TT
    scale = 96 ** -0.5
    with ExitStack() as s:
        sb = lambda name, shape, dt: s.enter_context(nc.sbuf_tensor(g.nm(name), shape, dt))
        KT = sb("KT", [96, 4, T], BF16)
        V = sb("V", [128, NC_, 260], BF16)
        Q = [sb("Q", [96, 4, TT], BF16) for _ in range(2)]
        pts = [sb("pt", [128, TT], BF16) for _ in range(3)]
        otile = sb("otile", [128, 4, 256], BF16)
        otile2 = sb("otile2", [128, 4, 256], BF16)
        oT = sb("oT", [128, 2, TT], BF16)
        rl = sb("rl", [128, 4], F32)
        t_K, t_V, t_rl, t_ot, t_oT, t_ot2 = Tok(), Tok(), Tok(), Tok(), Tok(), Tok()
        t_Q = [Tok(), Tok()]
        t_pts = [Tok() for _ in range(3)]
        for h in range(4):
            P.dma(lambda e, h=h: e.dma_start(out=KT[:, h, :], in_=S["mla_kT"][h]), reads=g.t_s["mla_kT"], writes=[t_K])
        for c0 in range(0, NC_, 4):
            P.dma(lambda e, c0=c0: e.dma_start(out=V[:, c0:c0 + 4, :], in_=S["mla_v"][c0 * 128:(c0 + 4) * 128, :].rearrange("(c p) d -> p c d", p=128)),
                  reads=g.t_s["mla_v"], writes=[t_V])
        sbanks = g.ps[0:3]
        accs = g.ps[3:5]

        def load_q(i):
            if i >= NQB:
                return
            tsl = slice(i * TT, (i + 1) * TT)
            for h in range(4):
                P.dma(lambda e, h=h: e.dma_start(out=Q[i % 2][:, h, :], in_=S["mla_qT"][h, :, tsl]), reads=[g.t_s["mla_qT"][i]], writes=[t_Q[i % 2]])

        hcount = 0
        blks = []
        k2ld = {}
        load_q(0)
        for i in range(NQB):
            qb = i % 2
            tsl = slice(i * TT, (i + 1) * TT)
            k2ld[len(blks)] = (lambda i=i: load_q(i + 1))
            specs = causal_chunks(i)
            for h in range(4):
                acc, acctok = accs[hcount % 2]
                hcount += 1
                accv = acc[:, :].rearrange("p (s c) -> p s c", c=128)
                pvs = mk_pv([(c, slo, shi) for (c, kd, o, qlo, qhi, slo, shi) in specs], accv, 65,
                            lambda c, h=h: (V[:, c, h * 65:(h + 1) * 65], t_V))
                for idx, (c, kd, o, qlo, qhi, slo, shi) in enumerate(specs):
                    b = Blk()
                    b.smm = [(KT[:, h, c * 128:(c + 1) * 128], Q[qb][:, h, qlo:qhi], [t_K, t_Q[qb]])]
                    if kd == "c":
                        b.smm.append((g.ident_bf[:, :], g.cbias[:, o, qlo:qhi], [g.t_const]))
                    b.qlo, b.qhi, b.scale = qlo, qhi, scale
                    b.pv = pvs[idx]
                    b.acctok = acctok
                    b.post = None
                    if idx == len(specs) - 1:
                        def post(h=h, accv=accv, acctok=acctok, tsl=tsl, i=i):
                            P.dve(lambda e: e.reciprocal(out=rl[:, :], in_=accv[:, :, 64]), reads=[acctok], writes=[t_rl])
                            for s_ in range(4):
                                P.dve(lambda e, s_=s_: e.tensor_scalar(out=otile[:, s_, h * 64:(h + 1) * 64], in0=accv[:, s_, 0:64], scalar1=rl[:, s_:s_ + 1], scalar2=None, op0=ALU.mult),
                                      reads=[acctok, t_rl], writes=[t_ot])
                            if h == 3:
                                def pe_part():
                                    import os
                                    if os.environ.get("NO_TR"):
                                        return
                                    for hf in range(2):
                                        for s_ in range(4):
                                            P.pe(lambda e, hf=hf, s_=s_: e.transpose(out=g.psT[:, hf, s_ * 128:(s_ + 1) * 128], in_=otile2[:, s_, hf * 128:(hf + 1) * 128], identity=g.ident_bf[:, :]),
                                                 reads=[t_ot2, g.t_const], writes=[g.t_psT])
                                    P.act(lambda e: e.copy(out=oT[:, :, :], in_=g.psT[:, :, :]), reads=[g.t_psT], writes=[t_oT])
                                    for hf in range(2):
                                        P.dma(lambda e, hf=hf: e.dma_start(out=S["oT"][hf * 128:(hf + 1) * 128, tsl], in_=oT[:, hf, :]), reads=[t_oT], writes=[g.t_s["oT"][i]])
                                P.pool(lambda e: e.tensor_copy(out=otile2[:, :, :], in_=otile[:, :, :]), reads=[t_ot], writes=[t_ot2])
                                return pe_part
                        b.post = post
                    blks.append(b)
        run_blocks_ld(P, g, blks, sbanks, pts, t_pts, k2ld)


def run_blocks_ld(P, g, blks, sbanks, pts, t_pts, k2ld):
    import os
    n = len(blks)
    nsb = len(sbanks)
    npt = len(pts)

    def S_(k):
        if k in k2ld:
            k2ld[k]()
        b = blks[k]
        bank, btok = sbanks[k % nsb]
        if len(b.smm) == 1 and os.environ.get("ZPAD", "0") == "1":
            b.smm.append((g.ident_bf[:, :], g.zbias[:, b.qlo:b.qhi], [g.t_const]))
        m = len(b.smm)
        for j, (lt, rh, rd) in enumerate(b.smm):
            M(P, bank[:, b.qlo:b.qhi], lt, rh, j == 0, j == m - 1, rd, [btok])

    def E_(k):
        b = blks[k]
        bank, btok = sbanks[k % nsb]
        i = k % npt
        P.act(lambda e: e.activation(out=pts[i][:, b.qlo:b.qhi], in_=bank[:, b.qlo:b.qhi], func=AF.Exp, scale=b.scale),
              reads=[btok], writes=[t_pts[i]])

    def V_(k):
        b = blks[k]
        i = k % npt
        for (out, s_, vap, vtok, st, sp) in b.pv:
            M(P, out, pts[i][:, s_ * 128:(s_ + 1) * 128], vap, st, sp, [t_pts[i], vtok], [b.acctok])
        if b.post is not None:
            r = b.post()
            if r is not None:
                for fn in (r if isinstance(r, (list, tuple)) else [r]):
                    deferred.append((k + DEFER, fn))
        while deferred and deferred[0][0] <= k:
            deferred.pop(0)[1]()

    DEFER = 3
    deferred = []
    if n == 0:
        return
    S_(0)
    if n > 1:
        S_(1)
    for k in range(n):
        E_(k)
        V_(k)
        if k + 2 < n:
            S_(k + 2)
    while deferred:
        deferred.pop(0)[1]()


def diff_pass(P, nc, g, T, l, W, lam_init):
    P.fence()
    S = g.S
    NC_ = T // 128
    NQB = T // TT
    scale = 32 ** -0.5
    with ExitStack() as s:
        sb = lambda name, shape, dt: s.enter_context(nc.sbuf_tensor(g.nm(name), shape, dt))
        KT = sb("KT", [96, 3, T], BF16)
        V = sb("V", [128, NC_, 260], BF16)
        Q = [sb("Q", [96, 3, TT], BF16) for _ in range(2)]
        pts = [sb("pt", [128, TT], BF16) for _ in range(3)]
        otile = sb("otile", [128, 4, 256], BF16)
        otile2 = sb("otile2", [128, 4, 256], BF16)
        t_ot2 = Tok()
        oT = sb("oT", [128, 2, TT], BF16)
        rl = sb("rl", [128, 2, 4], F32)
        o1 = sb("o1", [128, 4, 64], F32)
        dd = sb("dd", [128, 4, 64], F32)
        sqd = sb("sqd", [128, 4, 64], F32)
        ssq = sb("ssq", [128, 4], F32)
        dl = sb("dl", [128, 128], F32)
        gv = sb("gv", [128, 64], F32)
        lamt = sb("lamt", [128, 8], F32)
        t_K, t_V, t_rl, t_ot, t_oT, t_o1, t_dd, t_sqd, t_ssq, t_c = (Tok() for _ in range(10))
        t_Q = [Tok(), Tok()]
        t_pts = [Tok() for _ in range(3)]
        for j in range(3):
            P.dma(lambda e, j=j: e.dma_start(out=KT[:, j, :], in_=S["diff_kT"][j]), reads=g.t_s["diff_kT"], writes=[t_K])
        for c0 in range(0, NC_, 4):
            P.dma(lambda e, c0=c0: e.dma_start(out=V[:, c0:c0 + 4, :], in_=S["v_tm"][c0 * 128:(c0 + 4) * 128, 0:260].rearrange("(c p) d -> p c d", p=128)),
                  reads=g.t_s["v_tm"], writes=[t_V])
        P.dma(lambda e: e.dma_start(out=dl[:], in_=W["dlam"][l]), writes=[t_c])
        P.dma(lambda e: e.dma_start(out=gv[:], in_=W["dng"][l]), writes=[t_c])
        P.dve(lambda e: e.tensor_tensor(out=dl[:, 0:32], in0=dl[:, 0:32], in1=dl[:, 32:64], op=ALU.mult), reads=[t_c], writes=[t_c])
        P.dve(lambda e: e.tensor_tensor(out=dl[:, 64:96], in0=dl[:, 64:96], in1=dl[:, 96:128], op=ALU.mult), reads=[t_c], writes=[t_c])
        P.dve(lambda e: e.reduce_sum(out=lamt[:, 0:1], in_=dl[:, 0:32], axis=AX.X), reads=[t_c], writes=[t_c])
        P.dve(lambda e: e.reduce_sum(out=lamt[:, 1:2], in_=dl[:, 64:96], axis=AX.X), reads=[t_c], writes=[t_c])
        P.act(lambda e: e.activation(out=lamt[:, 2:4], in_=lamt[:, 0:2], func=AF.Exp), reads=[t_c], writes=[t_c])
        P.dve(lambda e: e.tensor_tensor(out=lamt[:, 4:5], in0=lamt[:, 3:4], in1=lamt[:, 2:3], op=ALU.subtract), reads=[t_c], writes=[t_c])
        P.dve(lambda e: e.tensor_scalar(out=lamt[:, 4:5], in0=lamt[:, 4:5], scalar1=-lam_init, scalar2=None, op0=ALU.add), reads=[t_c], writes=[t_c])
        P.dve(lambda e: e.tensor_scalar(out=gv[:], in0=gv[:], scalar1=1.0 - lam_init, scalar2=None, op0=ALU.mult), reads=[t_c], writes=[t_c])
        sbanks = g.ps[0:3]
        accs = g.ps[3:7]

        def load_q(i):
            if i >= NQB:
                return
            tsl = slice(i * TT, (i + 1) * TT)
            for j in range(3):
                P.dma(lambda e, j=j: e.dma_start(out=Q[i % 2][:, j, :], in_=S["diff_qT"][j, :, tsl]), reads=[g.t_s["diff_qT"][i]], writes=[t_Q[i % 2]])

        pcount = 0
        blks = []
        k2ld = {}
        load_q(0)
        for i in range(NQB):
            qb = i % 2
            tsl = slice(i * TT, (i + 1) * TT)
            k2ld[len(blks)] = (lambda i=i: load_q(i + 1))
            specs = causal_chunks(i)
            for h in range(4):
                accp = []
                for m in range(2):
                    p = 2 * h + m
                    j, base = p // 3, 32 * (p % 3)
                    acc, acctok = accs[pcount % 4]
                    pcount += 1
                    accv = acc[:, :].rearrange("p (s c) -> p s c", c=128)
                    accp.append((accv, acctok))
                    pvs = mk_pv([(c, slo, shi) for (c, kd, o, qlo, qhi, slo, shi) in specs], accv, 65,
                                lambda c, h=h: (V[:, c, h * 65:(h + 1) * 65], t_V))
                    for idx, (c, kd, o, qlo, qhi, slo, shi) in enumerate(specs):
                        b = Blk()
                        b.smm = [(KT[base:base + 32, j, c * 128:(c + 1) * 128], Q[qb][base:base + 32, j, qlo:qhi], [t_K, t_Q[qb]])]
                        if kd == "c":
                            b.smm.append((g.ident_bf[:, :], g.cbias[:, o, qlo:qhi], [g.t_const]))
                        b.qlo, b.qhi, b.scale = qlo, qhi, scale
                        b.pv = pvs[idx]
                        b.acctok = acctok
                        b.post = None
                        if idx == len(specs) - 1 and m == 1:
                            def post(h=h, accp=list(accp), tsl=tsl, i=i):
                                (a1, k1), (a2, k2) = accp
                                P.dve(lambda e: e.reciprocal(out=rl[:, 0, :], in_=a1[:, :, 64]), reads=[k1], writes=[t_rl])
                                P.dve(lambda e: e.reciprocal(out=rl[:, 1, :], in_=a2[:, :, 64]), reads=[k2], writes=[t_rl])
                                P.dve(lambda e: e.tensor_scalar(out=rl[:, 1, :], in0=rl[:, 1, :], scalar1=lamt[:, 4:5], scalar2=None, op0=ALU.mult), reads=[t_rl, t_c], writes=[t_rl])
                                for s_ in range(4):
                                    P.dve(lambda e, s_=s_: e.tensor_scalar(out=o1[:, s_, :], in0=a1[:, s_, 0:64], scalar1=rl[:, 0, s_:s_ + 1], scalar2=None, op0=ALU.mult),
                                          reads=[k1, t_rl], writes=[t_o1])
                                    P.dve(lambda e, s_=s_: e.scalar_tensor_tensor(out=dd[:, s_, :], in0=a2[:, s_, 0:64], scalar=rl[:, 1, s_:s_ + 1], in1=o1[:, s_, :], op0=ALU.mult, op1=ALU.add),
                                          reads=[k2, t_rl, t_o1], writes=[t_dd])
                                P.pool(lambda e: e.tensor_tensor(out=sqd[:], in0=dd[:], in1=dd[:], op=ALU.mult), reads=[t_dd], writes=[t_sqd])
                                P.dve(lambda e: e.reduce_sum(out=ssq[:], in_=sqd[:], axis=AX.X), reads=[t_sqd], writes=[t_ssq])
                                P.act(lambda e: e.activation(out=ssq[:], in_=ssq[:], func=AF.Ln, bias=g.eps[:, 0:1], scale=1.0 / 64), reads=[t_ssq, g.t_const], writes=[t_ssq])
                                P.act(lambda e: e.activation(out=ssq[:], in_=ssq[:], func=AF.Exp, scale=-0.5), reads=[t_ssq], writes=[t_ssq])
                                for s_ in range(4):
                                    P.dve(lambda e, s_=s_: e.scalar_tensor_tensor(out=otile[:, s_, h * 64:(h + 1) * 64], in0=dd[:, s_, :], scalar=ssq[:, s_:s_ + 1], in1=gv[:], op0=ALU.mult, op1=ALU.mult),
                                          reads=[t_dd, t_ssq, t_c], writes=[t_ot])
                                if h == 3:
                                    def pe_part():
                                        for hf in range(2):
                                            for s_ in range(4):
                                                P.pe(lambda e, hf=hf, s_=s_: e.transpose(out=g.psT[:, hf, s_ * 128:(s_ + 1) * 128], in_=otile2[:, s_, hf * 128:(hf + 1) * 128], identity=g.ident_bf[:, :]),
                                                     reads=[t_ot2, g.t_const], writes=[g.t_psT])
                                        P.act(lambda e: e.copy(out=oT[:, :, :], in_=g.psT[:, :, :]), reads=[g.t_psT], writes=[t_oT])
                                        for hf in range(2):
                                            P.dma(lambda e, hf=hf: e.dma_start(out=S["oT"][256 + hf * 128:256 + (hf + 1) * 128, tsl], in_=oT[:, hf, :]), reads=[t_oT], writes=[g.t_s["oT"][i]])
                                    P.pool(lambda e: e.tensor_copy(out=otile2[:, :, :], in_=otile[:, :, :]), reads=[t_ot], writes=[t_ot2])
                                    return pe_part
                            b.post = post
                        blks.append(b)
        run_blocks_ld(P, g, blks, sbanks, pts, t_pts, k2ld)


def nsa_pass(P, nc, g, T, l, W):
    import os
    P.fence()
    S = g.S
    NC_ = T // 128
    NQB = T // TT
    NBS = T // 64
    NCMP = T // 16 - 1
    NCH = T // 2048
    NR = NBS - 1
    WC = 65 + NR
    scale = 0.125
    with ExitStack() as s:
        sb = lambda name, shape, dt: s.enter_context(nc.sbuf_tensor(g.nm(name), shape, dt))
        Kslc = sb("Kslc", [128, T], BF16)
        Kwin = sb("Kwin", [128, T], BF16)
        Vs = sb("Vs", [128, NC_, 260], BF16)
        KcT = sb("KcT", [128, NCH * 128], BF16)
        Vc = sb("Vc", [128, NCH, 2, 128], BF16)
        ebig = sb("ebig", [128, T], BF16)
        t_K, t_V, t_Kc, t_Vc, t_c = (Tok() for _ in range(5))
        P.dma(lambda e: e.dma_start(out=Kslc[:], in_=S["nsa_kT"][1]), reads=g.t_s["nsa_kT"], writes=[t_K])
        P.dma(lambda e: e.dma_start(out=Kwin[:], in_=S["nsa_kT"][2]), reads=g.t_s["nsa_kT"], writes=[t_K])
        for c0 in range(0, NC_, 4):
            P.dma(lambda e, c0=c0: e.dma_start(out=Vs[:, c0:c0 + 4, :], in_=S["v_tm"][c0 * 128:(c0 + 4) * 128, 260:520].rearrange("(c p) d -> p c d", p=128)),
                  reads=g.t_s["v_tm"], writes=[t_V])
        P.dma(lambda e: e.dma_start(out=ebig[:], in_=W["ebig"]), writes=[t_c])
        P.dve(lambda e: e.memset(KcT[:], 0.0), writes=[t_Kc])
        P.dve(lambda e: e.memset(Vc[:], 0.0), writes=[t_Vc])
        with ExitStack() as s2:
            sb2 = lambda name, shape, dt: s2.enter_context(nc.sbuf_tensor(g.nm(name), shape, dt))
            src = [sb2("csrc", [128, T], BF16) for _ in range(2)]
            w1 = [sb2("cw1", [128, 32, 256], BF16) for _ in range(2)]
            w2 = [sb2("cw2", [128, 2, 128], BF16) for _ in range(2)]
            posT = sb2("cpos", [128, 2, 32], BF16)
            b1 = sb2("cb1", [128, 2, 2], F32)
            btot = sb2("btot", [128, 2, 2], F32)
            t_src, t_cw, t_btot = Tok(), Tok(), Tok()
            P.dma(lambda e: e.dma_start(out=src[0][:], in_=S["nsa_kT"][0]), reads=g.t_s["nsa_kT"], writes=[t_src])
            P.dma(lambda e: e.dma_start(out=src[1][:], in_=S["nsa_vcT"]), reads=g.t_s["nsa_vcT"], writes=[t_src])
            for kv in range(2):
                P.dma(lambda e, kv=kv: e.dma_start(out=w1[kv][:], in_=W["cw1"][l, kv].rearrange("p (l f) -> p l f", f=256)), writes=[t_cw], q="pool")
                P.dma(lambda e, kv=kv: e.dma_start(out=w2[kv][:], in_=W["cw2"][l, kv].rearrange("p (c f) -> p c f", f=128)), writes=[t_cw], q="pool")
                P.dma(lambda e, kv=kv: e.dma_start(out=posT[:, kv, :], in_=W["cpos"][l, kv]), writes=[t_cw], q="pool")
                P.dma(lambda e, kv=kv: e.dma_start(out=b1[:, kv, :], in_=W["cb1"][l, kv]), writes=[t_cw])
            P.dma(lambda e: e.dma_start(out=Vc[:, :, 0, 65:65 + NR], in_=W["mapm"].rearrange("(c p) j -> p c j", p=128)), reads=[], writes=[t_Vc])
            P.dma(lambda e: e.dma_start(out=Vc[:, :, 1, 65:65 + NR], in_=W["mapm"].rearrange("(c p) j -> p c j", p=128)), reads=[], writes=[t_Vc])
            P.dve(lambda e: e.memset(Vc[:, :, :, 64:65], 1.0), reads=[], writes=[t_Vc])
            hids = {(kv, gg): sb2("hid", [128, 2, NCH * 128], BF16) for kv in range(2) for gg in range(2)}
            t_hids = {k_: Tok() for k_ in hids}
            for kv in range(2):
                for fc in range(2):
                    pst, ptok = nextps(g)
                    for ll in range(32):
                        M(P, pst[:, 0:1], w1[kv][0:64, ll, fc * 128:(fc + 1) * 128], posT[0:64, kv, ll:ll + 1], ll == 0, ll == 31, [t_cw], [ptok])
                    P.dve(lambda e, kv=kv, fc=fc, pst=pst: e.tensor_tensor(out=btot[:, kv, fc:fc + 1], in0=pst[:, 0:1], in1=b1[:, kv, fc:fc + 1], op=ALU.add),
                          reads=[ptok, t_cw], writes=[t_btot])
            for kv in range(2):
                for gg in range(2):
                    r0 = 64 * gg
                    hid_ = hids[(kv, gg)]
                    for fc in range(2):
                        pst, ptok = nextps(g)
                        for ll in range(32):
                            M(P, pst[:, 0:NCMP], w1[kv][r0:r0 + 64, ll, fc * 128:(fc + 1) * 128],
                              src[kv][r0:r0 + 64, ll:ll + 16 * (NCMP - 1) + 1:16], ll == 0, ll == 31, [t_cw, t_src], [ptok])
                        P.act(lambda e, kv=kv, fc=fc, pst=pst, hid_=hid_: e.activation(out=hid_[:, fc, 0:NCMP], in_=pst[:, 0:NCMP], func=AF.Silu, bias=btot[:, kv, fc:fc + 1]),
                              reads=[ptok, t_btot], writes=[t_hids[(kv, gg)]])
            for kv in range(2):
                for gg in range(2):
                    hid_ = hids[(kv, gg)]
                    t_hid = t_hids[(kv, gg)]
                    if kv == 0:
                        pst, ptok = nextps(g)
                        if gg == 0:
                            for fc in range(2):
                                M(P, pst[0:64, 0:NCMP], w2[0][:, fc, 64:128], hid_[:, fc, 0:NCMP], fc == 0, fc == 1, [t_cw, t_hid], [ptok])
                            P.act(lambda e, pst=pst: e.copy(out=KcT[0:64, 0:NCMP], in_=pst[0:64, 0:NCMP]), reads=[ptok], writes=[t_Kc])
                        else:
                            for fc in range(2):
                                M(P, pst[:, 0:NCMP], w2[0][:, fc, 0:128], hid_[:, fc, 0:NCMP], fc == 0, fc == 1, [t_cw, t_hid], [ptok])
                            P.act(lambda e, pst=pst: e.copy(out=KcT[64:128, 0:NCMP], in_=pst[64:128, 0:NCMP]), reads=[ptok], writes=[t_Kc])
                    else:
                        for nch in range(NCH):
                            m = min(128, NCMP - nch * 128)
                            pst, ptok = nextps(g)
                            for fc in range(2):
                                M(P, pst[0:m, 0:64], hid_[:, fc, nch * 128:nch * 128 + m], w2[1][:, fc, 64:128], fc == 0, fc == 1, [t_cw, t_hid], [ptok])
                            P.act(lambda e, pst=pst, m=m, nch=nch, gg=gg: e.copy(out=Vc[0:m, nch, gg, 0:64], in_=pst[0:m, 0:64]), reads=[ptok], writes=[t_Vc])
        P.fence()
        Q = [sb("Q", [128, 4, TT], BF16) for _ in range(2)]
        cmpb = [sb("cmpb", [128, NCH, TT], BF16) for _ in range(2)]
        gat = [sb("gat", [128, 4, 24], F32) for _ in range(3)]
        vmt = [sb("vmt", [128, 4, NBS], F32) for _ in range(3)]
        adt = [sb("adt", [128, 4, NBS], F32) for _ in range(3)]
        t_G = [Tok() for _ in range(3)]
        t_Gg = [Tok() for _ in range(3)]
        pts = [sb("pt", [128, TT], BF16) for _ in range(3)]
        ocomb = sb("ocomb", [128, 4, 512], F32)
        otile = sb("otile", [128, 4, 512], BF16)
        oT = sb("oT", [128, 4, TT], BF16)
        rl = sb("rl", [128, 4], F32)
        wgt = sb("wgt", [128, 4], F32)
        pslc = sb("pslc", [128, 2, 4, NBS], F32)
        score = sb("score", [128, 4, NBS], F32)
        work = sb("work", [128, NBS], F32)
        mx = sb("mx", [128, 16], F32)
        selm1 = [sb("selm1", [128, 4, NBS], F32) for _ in range(2)]
        selT = [sb("selT", [128, TT], BF16) for _ in range(2)]
        t_Q = [Tok(), Tok()]
        t_pts = [Tok() for _ in range(3)]
        t_oc, t_ot, t_oT, t_rl, t_wgt, t_score, t_work, t_mx = (Tok() for _ in range(8))
        t_selm1 = [Tok(), Tok()]
        t_pslc = [Tok(), Tok()]
        t_selT = [Tok(), Tok()]
        for gg_ in range(2):
            P.dve(lambda e, gg_=gg_: e.memset(selT[gg_][:], 0.0), writes=[t_selT[gg_]])
        sbanks = g.ps[0:3]
        accs = g.ps[3:5]
        misc = g.ps[5:7]

        def load_q(i):
            if i >= NQB:
                return
            tsl = slice(i * TT, (i + 1) * TT)
            b_ = i % 2
            for j in range(4):
                P.dma(lambda e, j=j: e.dma_start(out=Q[b_][:, j, :], in_=S["nsa_qT"][j, :, tsl]), reads=[g.t_s["nsa_qT"][i]], writes=[t_Q[b_]])
            P.dma(lambda e: e.dma_start(out=cmpb[b_][:], in_=W["cmpbias"][:, :, tsl].rearrange("c p t -> p c t")), writes=[t_Q[b_]])
            b3 = i % 3
            P.dma(lambda e: e.dma_start(out=gat[b3][:], in_=S["gates"][tsl, :].rearrange("(s p) c -> p s c", p=128)), reads=[g.t_s["gates"][i]], writes=[t_Gg[b3]])
            P.act(lambda e: e.activation(out=gat[b3][:], in_=gat[b3][:], func=AF.Exp, scale=-1.0), reads=[t_Gg[b3]], writes=[t_Gg[b3]])
            P.dve(lambda e: e.tensor_scalar(out=gat[b3][:], in0=gat[b3][:], scalar1=1.0, scalar2=None, op0=ALU.add), reads=[t_Gg[b3]], writes=[t_Gg[b3]])
            P.dve(lambda e: e.reciprocal(out=gat[b3][:], in_=gat[b3][:]), reads=[t_Gg[b3]], writes=[t_Gg[b3]])
            P.dma(lambda e: e.dma_start(out=vmt[b3][:], in_=W["validm"][tsl, :].rearrange("(s p) c -> p s c", p=128)), writes=[t_G[b3]])
            P.dma(lambda e: e.dma_start(out=adt[b3][:], in_=W["addc"][tsl, :].rearrange("(s p) c -> p s c", p=128)), writes=[t_G[b3]])

        hcount = 0
        blks = []
        k2ld = {}
        load_q(0)
        for i in range(NQB):
            qb = i % 2
            q3 = i % 3
            tsl = slice(i * TT, (i + 1) * TT)
            k2ld[len(blks)] = (lambda i=i: load_q(i + 1))

            def finish(tsl=tsl, i=i):
                P.act(lambda e: e.copy(out=otile[:, :, :], in_=ocomb[:, :, :]), reads=[t_oc], writes=[t_ot])

                def pe_part():
                    for pr in range(2):
                        for hh in range(2):
                            cb = pr * 2 + hh
                            for s_ in range(4):
                                P.pe(lambda e, cb=cb, hh=hh, s_=s_: e.transpose(out=g.psT[:, hh, s_ * 128:(s_ + 1) * 128], in_=otile[:, s_, cb * 128:(cb + 1) * 128], identity=g.ident_bf[:, :]),
                                     reads=[t_ot, g.t_const], writes=[g.t_psT])
                        P.act(lambda e, pr=pr: e.copy(out=oT[:, pr * 2:pr * 2 + 2, :], in_=g.psT[:, :, :]), reads=[g.t_psT], writes=[t_oT])
                    for cb in range(4):
                        P.dma(lambda e, cb=cb: e.dma_start(out=S["oT"][512 + cb * 128:512 + (cb + 1) * 128, tsl], in_=oT[:, cb, :]), reads=[t_oT], writes=[g.t_s["oT"][i]])
                return pe_part

            BRS = os.environ.get("NSA_BR", "012")
            inited = set()

            def epilogue(br, h, accv, acctok, qb=qb, q3=q3, first=False, inited=inited, BRS=BRS, finish=finish):
                if str(br) not in BRS:
                    if br == 0:
                        P.dve(lambda e: e.tensor_scalar(out=rl[:, :], in0=accv[:, :, 64], scalar1=1e-30, scalar2=None, op0=ALU.add), reads=[acctok], writes=[t_rl])
                        P.dve(lambda e: e.reciprocal(out=rl[:, :], in_=rl[:, :]), reads=[t_rl], writes=[t_rl])
                    return
                first = h not in inited
                inited.add(h)
                last_br = 1 if "1" in BRS else (2 if "2" in BRS else 0)
                P.dve(lambda e: e.tensor_scalar(out=rl[:, :], in0=accv[:, :, 64], scalar1=1e-30, scalar2=None, op0=ALU.add), reads=[acctok], writes=[t_rl])
                P.dve(lambda e: e.reciprocal(out=rl[:, :], in_=rl[:, :]), reads=[t_rl], writes=[t_rl])
                P.dve(lambda e: e.tensor_tensor(out=wgt[:, :], in0=rl[:, :], in1=gat[q3][:, :, h * 3 + br], op=ALU.mult), reads=[t_rl, t_Gg[q3]], writes=[t_wgt])
                for s_ in range(4):
                    if first:
                        P.dve(lambda e, s_=s_: e.tensor_scalar(out=ocomb[:, s_, h * 64:(h + 1) * 64], in0=accv[:, s_, 0:64], scalar1=wgt[:, s_:s_ + 1], scalar2=None, op0=ALU.mult),
                              reads=[acctok, t_wgt], writes=[t_oc])
                    else:
                        P.dve(lambda e, s_=s_: e.scalar_tensor_tensor(out=ocomb[:, s_, h * 64:(h + 1) * 64], in0=accv[:, s_, 0:64], scalar=wgt[:, s_:s_ + 1], in1=ocomb[:, s_, h * 64:(h + 1) * 64], op0=ALU.mult, op1=ALU.add),
                              reads=[acctok, t_wgt, t_oc], writes=[t_oc])
                if br == last_br and h == 7 and br != 0:
                    return finish()
                return None

            def select(gg, qb=qb, q3=q3):
                P.dve(lambda e: e.tensor_tensor(out=score[:], in0=pslc[:, gg, :, :], in1=vmt[q3][:], op=ALU.mult), reads=[t_pslc[gg], t_G[q3]], writes=[t_score])
                P.dve(lambda e: e.tensor_tensor(out=score[:], in0=score[:], in1=adt[q3][:], op=ALU.add), reads=[t_score, t_G[q3]], writes=[t_score])
                pst, ptok = misc[gg]
                for s_ in range(4):
                    P.dve(lambda e, s_=s_: e.max(out=mx[:, 0:8], in_=score[:, s_, :]), reads=[t_score], writes=[t_mx])
                    P.dve(lambda e, s_=s_: e.match_replace(out=work[:], in_to_replace=mx[:, 0:8], in_values=score[:, s_, :], imm_value=-2.0), reads=[t_score, t_mx], writes=[t_work])
                    P.dve(lambda e: e.max(out=mx[:, 8:16], in_=work[:]), reads=[t_work], writes=[t_mx])
                    P.dve(lambda e, s_=s_: e.tensor_scalar(out=selm1[gg][:, s_, :], in0=score[:, s_, :], scalar1=mx[:, 15:16], scalar2=-1.0, op0=ALU.is_ge, op1=ALU.add),
                          reads=[t_score, t_mx], writes=[t_selm1[gg]])

                def pe_part():
                    for s_ in range(4):
                        P.pe(lambda e, s_=s_, pst=pst: e.transpose(out=pst[0:NBS, s_ * 128:(s_ + 1) * 128], in_=selm1[gg][:, s_, :], identity=g.ident_f[:, :]),
                             reads=[t_selm1[gg], g.t_const], writes=[ptok])
                    P.act(lambda e, pst=pst: e.copy(out=selT[gg][0:NBS, :], in_=pst[0:NBS, :]), reads=[ptok], writes=[t_selT[gg]])
                return pe_part

            nchs = [n_ for n_ in range(NCH) if 2048 * n_ + 31 <= 512 * i + 511]
            for h in range(8):
                gg, j = h // 4, h % 4
                r0 = 64 * gg
                acc, acctok = accs[hcount % 2]
                hcount += 1
                accv = acc[:, :].rearrange("p (s c) -> p s c", c=128)
                pvs = mk_pv([(n_, 0, 4) for n_ in nchs], accv, WC, lambda n_, gg=gg: (Vc[:, n_, gg, 0:WC], t_Vc))
                for idx, n_ in enumerate(nchs):
                    b = Blk()
                    b.smm = [(KcT[r0:r0 + 64, n_ * 128:(n_ + 1) * 128], Q[qb][r0:r0 + 64, j, :], [t_Kc, t_Q[qb]]),
                             (g.ident_bf[:, :], cmpb[qb][:, n_, :], [g.t_const, t_Q[qb]])]
                    b.qlo, b.qhi, b.scale = 0, 512, scale
                    b.pv = pvs[idx]
                    b.acctok = acctok
                    b.post = None
                    if idx == len(nchs) - 1:
                        def post(h=h, gg=gg, accv=accv, acctok=acctok, epilogue=epilogue, select=select, finish=finish):
                            ret = []
                            epilogue(0, h, accv, acctok, first=True)
                            if h % 4 == 0:
                                P.dve(lambda e: e.memset(pslc[:, gg, :, :], 0.0), writes=[t_pslc[gg]])
                            for s_ in range(4):
                                P.dve(lambda e, s_=s_: e.scalar_tensor_tensor(out=pslc[:, gg, s_, 0:NR], in0=accv[:, s_, 65:65 + NR], scalar=rl[:, s_:s_ + 1], in1=pslc[:, gg, s_, 0:NR], op0=ALU.mult, op1=ALU.add),
                                      reads=[acctok, t_rl, t_pslc[gg]], writes=[t_pslc[gg]])
                            if h % 4 == 3:
                                ret.append(select(gg))
                            if h == 7 and BRS == "0":
                                ret.append(finish())
                            return ret
                        b.post = post
                    blks.append(b)
            wspecs = win_chunks(i)
            for h in (range(8) if "2" in BRS else []):
                gg, j = h // 4, h % 4
                r0 = 64 * gg
                acc, acctok = accs[hcount % 2]
                hcount += 1
                accv = acc[:, :].rearrange("p (s c) -> p s c", c=128)
                pvs = mk_pv([(c, slo, shi) for (c, kd, o, qlo, qhi, slo, shi) in wspecs], accv, 65, lambda c, gg=gg: (Vs[:, c, (2 + gg) * 65:(3 + gg) * 65], t_V))
                for idx, (c, kd, o, qlo, qhi, slo, shi) in enumerate(wspecs):
                    b = Blk()
                    mb = g.cbias if kd == "c" else g.lbias
                    b.smm = [(Kwin[r0:r0 + 64, c * 128:(c + 1) * 128], Q[qb][r0:r0 + 64, j, qlo:qhi], [t_K, t_Q[qb]]),
                             (g.ident_bf[:, :], mb[:, o, qlo:qhi], [g.t_const])]
                    b.qlo, b.qhi, b.scale = qlo, qhi, scale
                    b.pv = pvs[idx]
                    b.acctok = acctok
                    b.post = None
                    if idx == len(wspecs) - 1:
                        b.post = (lambda h=h, accv=accv, acctok=acctok, epilogue=epilogue: epilogue(2, h, accv, acctok))
                    blks.append(b)
            specs = causal_chunks(i)
            for h in (range(8) if "1" in BRS else []):
                gg, j = h // 4, h % 4
                r0 = 64 * gg
                acc, acctok = accs[hcount % 2]
                hcount += 1
                accv = acc[:, :].rearrange("p (s c) -> p s c", c=128)
                pvs = mk_pv([(c, slo, shi) for (c, kd, o, qlo, qhi, slo, shi) in specs], accv, 65, lambda c, gg=gg: (Vs[:, c, gg * 65:(gg + 1) * 65], t_V))
                for idx, (c, kd, o, qlo, qhi, slo, shi) in enumerate(specs):
                    b = Blk()
                    b.smm = [(Kslc[r0:r0 + 64, c * 128:(c + 1) * 128], Q[qb][r0:r0 + 64, j, qlo:qhi], [t_K, t_Q[qb]]),
                             (ebig[:, c * 128:(c + 1) * 128], selT[gg][:, qlo:qhi], [t_c, t_selT[gg]])]
                    if kd == "c":
                        b.smm.append((g.ident_bf[:, :], g.cbias[:, o, qlo:qhi], [g.t_const]))
                    b.qlo, b.qhi, b.scale = qlo, qhi, scale
                    b.pv = pvs[idx]
                    b.acctok = acctok
                    b.post = None
                    if idx == len(specs) - 1:
                        def post(h=h, accv=accv, acctok=acctok, tsl=tsl, i=i, epilogue=epilogue):
                            return epilogue(1, h, accv, acctok)
                        b.post = post
                    blks.append(b)
        run_blocks_ld(P, g, blks, sbanks, pts, t_pts, k2ld)


def wout_pass(P, nc, g, T, l, W):
    P.fence()
    S = g.S
    with ExitStack() as s:
        sb = lambda name, shape, dt: s.enter_context(nc.sbuf_tensor(g.nm(name), shape, dt))
        wo = sb("wo", [128, 8, D], BF16)
        xt = [sb("xt", [128, 8, TT], F32) for _ in range(2)]
        ot = [sb("ot", [128, 8, TT], BF16) for _ in range(2)]
        t_w = Tok()
        t_x = [Tok(), Tok()]
        t_o = [Tok(), Tok()]
        for c in range(8):
            P.dma(lambda e, c=c: e.dma_start(out=wo[:, c, :], in_=W["w_out"][l, c * 128:(c + 1) * 128, :]), writes=[t_w], q="pool")
        xs_v = g.xs.rearrange("(c p) t -> p c t", p=128)
        os_v = S["oT"].rearrange("(c p) t -> p c t", p=128)
        for it in range(T // TT):
            b_ = it % 2
            tsl = slice(it * TT, (it + 1) * TT)
            P.dma(lambda e, tsl=tsl, b_=b_: e.dma_start(out=xt[b_][:], in_=xs_v[:, :, tsl]), reads=[g.t_x[it]], writes=[t_x[b_]])
            P.dma(lambda e, tsl=tsl, b_=b_: e.dma_start(out=ot[b_][:], in_=os_v[:, :, tsl]), reads=[g.t_s["oT"][it]], writes=[t_o[b_]])
            for c in range(8):
                pst, ptok = nextps(g)
                for k in range(8):
                    M(P, pst[:, :], wo[:, k, c * 128:(c + 1) * 128], ot[b_][:, k, :], k == 0, k == 7, [t_w, t_o[b_]], [ptok])
                P.dve(lambda e, c=c, pst=pst, b_=b_: e.tensor_tensor(out=xt[b_][:, c, :], in0=pst[:, :], in1=xt[b_][:, c, :], op=ALU.add),
                      reads=[ptok, t_x[b_]], writes=[t_x[b_]])
            P.dma(lambda e, tsl=tsl, b_=b_: e.dma_start(out=xs_v[:, :, tsl], in_=xt[b_][:]), reads=[t_x[b_]], writes=[g.t_x[it]])


def w_specs(T):
    L = L_DEPTH
    NBS, NCH = T // 64, max(1, T // 2048)
    NR = NBS - 1
    f, b = "f32", "bf16"
    return {
        "ffn_g": ([L, 2, 128, 8], f), "mix_g": ([L, 128, 8], f), "fin_g": ([128, 8], f),
        "ffn_wg": ([L, 2, D, DFF], f), "ffn_wu": ([L, 2, D, DFF], f), "ffn_wd": ([L, 2, DFF, D], f),
        "w_inA": ([L, D, NWA], f), "w_uq": ([L, 256, 384], f), "w_ukvA": ([L, 128, 640], f),
        "qn_g": ([L, 128, 2], f), "kvn_g": ([L, 128, 1], f), "gate_b": ([L, 128, 24], f),
        "dlam": ([L, 128, 128], f), "dng": ([L, 128, 64], f),
        "cpos": ([L, 2, 128, 32], f), "cw1": ([L, 2, 128, 8192], f), "cb1": ([L, 2, 128, 2], f), "cw2": ([L, 2, 128, 256], f),
        "w_out": ([L, D, D], f),
        "rt_mla_c": ([96, T], f), "rt_mla_s": ([96, T], f), "rt_diff_c": ([96, T], f), "rt_diff_s": ([96, T], f),
        "rt_nsa_c": ([128, T], f), "rt_nsa_s": ([128, T], f),
        "w_inS": ([L, D, NSW], f), "w_uqS": ([L, 256, 384], f),
        "ident": ([128, 128], f), "cbias": ([128, 4, 512], b), "lbias": ([128, 4, 512], b),
        "cmpbias": ([NCH, 128, T], b), "ebig": ([128, T], b), "mapm": ([NCH * 128, NR], b),
        "validm": ([T, NBS], f), "addc": ([T, NBS], f),
    }


def s_specs(T):
    b = BF16
    return {
        "mla_qT": ([4, 96, T], b), "mla_kT": ([4, 96, T], b), "mla_v": ([T, 260], b),
        "diff_qT": ([3, 96, T], b), "diff_kT": ([3, 96, T], b),
        "nsa_qT": ([4, 128, T], b), "nsa_kT": ([3, 128, T], b), "nsa_vcT": ([128, T], b),
        "v_tm": ([T, 520], b), "gates": ([T, 24], F32), "oT": ([D, T], b),
    }


def build(T, stop_after=None, debug=False):
    nc = bass.Bass("TRN2", target_bir_lowering=False)
    dtm = {"f32": F32, "bf16": BF16}
    W = {k: nc.dram_tensor(k, shp, dtm[dt], kind="ExternalInput").ap() for k, (shp, dt) in w_specs(T).items()}
    xT = nc.dram_tensor("xT", [D, T], F32, kind="ExternalInput").ap()
    out = nc.dram_tensor("out", [D, T], F32, kind="ExternalOutput").ap()
    skind = "ExternalOutput" if debug else "Internal"
    g = G()
    g.S = {k: nc.dram_tensor("s_" + k, shp, dt, kind=skind).ap() for k, (shp, dt) in s_specs(T).items()}
    g.xs = nc.dram_tensor("s_xs", [D, T], F32, kind=skind).ap()
    NT = T // TT
    g.t_x = [Tok() for _ in range(NT)]
    g.t_s = {k: [Tok() for _ in range(NT)] for k in g.S}
    g.t_out = Tok()
    cnt = [0]

    def nm(base):
        cnt[0] += 1
        return f"sb{cnt[0]}_{base}"
    g.nm = nm
    g.pi = 0
    P = Prog(nc)
    with ExitStack() as st:
        g.ps = [(st.enter_context(nc.psum_tensor(f"ps{i}", [128, 512], F32)), Tok()) for i in range(7)]
        g.psT = st.enter_context(nc.psum_tensor("psT", [128, 2, 512], BF16))
        g.t_psT = Tok()
        g.ones_bf = st.enter_context(nc.sbuf_tensor("c_ones", [128, 128], BF16))
        g.ident_bf = st.enter_context(nc.sbuf_tensor("c_identb", [128, 128], BF16))
        g.ident_f = st.enter_context(nc.sbuf_tensor("c_identf", [128, 128], F32))
        g.eps = st.enter_context(nc.sbuf_tensor("c_eps", [128, 1], F32))
        g.cbias = st.enter_context(nc.sbuf_tensor("c_cbias", [128, 4, 512], BF16))
        g.lbias = st.enter_context(nc.sbuf_tensor("c_lbias", [128, 4, 512], BF16))
        g.t_const = Tok()
        g.zbias = st.enter_context(nc.sbuf_tensor("c_zbias", [128, 512], BF16))
        P.dve(lambda e: e.memset(g.zbias[:], 0.0), writes=[g.t_const])
        P.dve(lambda e: e.memset(g.ones_bf[:], 1.0), writes=[g.t_const])
        P.dve(lambda e: e.memset(g.eps[:], EPS), writes=[g.t_const])
        P.dma(lambda e: e.dma_start(out=g.ident_bf[:], in_=W["ident"]), writes=[g.t_const], q="pool")
        P.dma(lambda e: e.dma_start(out=g.ident_f[:], in_=W["ident"]), writes=[g.t_const])
        P.dma(lambda e: e.dma_start(out=g.cbias[:], in_=W["cbias"]), writes=[g.t_const])
        P.dma(lambda e: e.dma_start(out=g.lbias[:], in_=W["lbias"]), writes=[g.t_const])
        steps = []
        for l in range(L_DEPTH):
            lam_init = 0.8 - 0.6 * math.exp(-0.3 * l)
            last = (l == L_DEPTH - 1)
            steps.append(("ffn1", lambda l=l: ffn_pass(P, nc, g, T, xT if l == 0 else g.xs, g.xs, W["ffn_g"][l, 0], W["ffn_wg"][l, 0], W["ffn_wu"][l, 0], W["ffn_wd"][l, 0])))
            steps.append(("proj", lambda l=l: proj_pass(P, nc, g, T, l, W)))
            steps.append(("mla", lambda l=l: mla_pass(P, nc, g, T, l, W)))
            steps.append(("diff", lambda l=l, lam_init=lam_init: diff_pass(P, nc, g, T, l, W, lam_init)))
            steps.append(("nsa", lambda l=l: nsa_pass(P, nc, g, T, l, W)))
            steps.append(("wout", lambda l=l: wout_pass(P, nc, g, T, l, W)))
            if last:
                steps.append(("ffn2", lambda l=l: ffn_pass(P, nc, g, T, g.xs, g.xs, W["ffn_g"][l, 1], W["ffn_wg"][l, 1], W["ffn_wu"][l, 1], W["ffn_wd"][l, 1], fin_g=W["fin_g"], out_dst=out)))
            else:
                steps.append(("ffn2", lambda l=l: ffn_pass(P, nc, g, T, g.xs, g.xs, W["ffn_g"][l, 1], W["ffn_wg"][l, 1], W["ffn_wu"][l, 1], W["ffn_wd"][l, 1])))
        for si, (name, fn) in enumerate(steps):
            fn()
            g.pi = 0
            if stop_after is not None and si == stop_after:
                break
        nw = P.emit(st)
        g.nops, g.nw = len(P.ops), nw
        g.P = P
    return nc, g


def _rope_tab(T, dim):
    inv = (10000.0 ** (-np.arange(0, dim, 2, dtype=np.float32) / np.float32(dim))).astype(np.float32)
    ang = np.arange(T, dtype=np.float32)[:, None] * inv[None, :]
    return np.cos(ang).astype(np.float32).T, np.sin(ang).astype(np.float32).T


def host_consts(T):
    bf = ml_dtypes.bfloat16
    NBS, NCH = T // 64, max(1, T // 2048)
    NCMP = T // 16 - 1
    NR = NBS - 1
    C = {}
    c32, s32 = _rope_tab(T, 32)
    c64, s64 = _rope_tab(T, 64)
    mc = np.ones((96, T), np.float32); ms = np.zeros((96, T), np.float32)
    for r in range(32):
        mc[64 + r] = c32[r % 16]; ms[64 + r] = s32[r % 16] * (-1.0 if r < 16 else 1.0)
    dc = np.zeros((96, T), np.float32); ds = np.zeros((96, T), np.float32)
    for r in range(96):
        dc[r] = c32[(r % 32) % 16]; ds[r] = s32[(r % 32) % 16] * (-1.0 if (r % 32) < 16 else 1.0)
    ncn = np.zeros((128, T), np.float32); nsn = np.zeros((128, T), np.float32)
    for r in range(128):
        ncn[r] = c64[(r % 64) % 32]; nsn[r] = s64[(r % 64) % 32] * (-1.0 if (r % 64) < 32 else 1.0)
    C.update(rt_mla_c=mc, rt_mla_s=ms, rt_diff_c=dc, rt_diff_s=ds, rt_nsa_c=ncn, rt_nsa_s=nsn)

    C["ident"] = np.eye(128, dtype=np.float32)
    kk = np.arange(128)[:, None]
    qq = np.arange(512)[None, :]
    cb = np.zeros((128, 4, 512), np.float32); lb = np.zeros((128, 4, 512), np.float32)
    for o in range(4):
        cb[:, o, :] = np.where(128 * o + kk <= qq, 0.0, NEGB)
        lb[:, o, :] = np.where(qq < 128 * o + kk, 0.0, NEGB)
    C["cbias"] = cb.astype(bf); C["lbias"] = lb.astype(bf)
    n = np.arange(NCH * 128)[:, None]
    t = np.arange(T)[None, :]
    cmpb = np.where((16 * n + 31 <= t) & (n < NCMP), 0.0, NEGB).astype(np.float32)
    C["cmpbias"] = cmpb.reshape(NCH, 128, T).astype(bf)
    j = np.arange(NBS)[:, None]
    eb = np.zeros((128, T), np.float32)
    eb[:NBS] = np.where((t // 64) == j, -NEGB, 0.0)
    C["ebig"] = eb.astype(bf)
    mp = np.zeros((NCH * 128, NBS), np.float32)
    for jj in range(NBS):
        for m_ in range(4):
            for n_ in range(2):
                idx = 4 * jj - m_ - n_
                if 0 <= idx < NCMP:
                    mp[idx, jj] += 1.0
    C["mapm"] = mp[:, :NR].astype(bf)
    tp = np.arange(T)[:, None]
    jb = np.arange(NBS)[None, :]
    cblk = tp // 64
    valid = jb <= cblk
    forced = (jb == 0) | (jb == cblk) | (jb == cblk - 1)
    C["validm"] = valid.astype(np.float32)
    C["addc"] = np.where(valid, np.where(forced, 1e4, 0.0), -1.0).astype(np.float32)
    return C


def host_weights(inp):
    L = L_DEPTH
    f = np.float32
    Wd = {}
    pc = lambda v: np.ascontiguousarray(v.reshape(-1, 128).T)
    Wd["ffn_g"] = np.stack([np.stack([pc(inp["ffn_norm_g"][l, i]) for i in range(2)]) for l in range(L)]).astype(f)
    Wd["mix_g"] = np.stack([pc(inp["mix_norm_g"][l]) for l in range(L)]).astype(f)
    Wd["fin_g"] = pc(inp["final_norm_g"]).astype(f)
    Wd["ffn_wg"] = np.ascontiguousarray(inp["ffn_w_gate"], dtype=f)
    Wd["ffn_wu"] = np.ascontiguousarray(inp["ffn_w_up"], dtype=f)
    Wd["ffn_wd"] = np.ascontiguousarray(inp["ffn_w_down"], dtype=f)
    wA = np.zeros((L, D, NWA), f)
    wukvA = np.zeros((L, 128, 640), f)
    for l in range(L):
        w = np.asarray(inp["w_in"][l], dtype=f)
        cq, ckv, kr, dq, dk, dv, nq, nkv, ng = np.split(w, np.cumsum([256, 128, 32, 256, 256, 256, 512, 768])[:8].tolist(), axis=1)
        wA[l, :, O_CQ:O_CQ + 256] = cq
        wA[l, :, O_CKV:O_CKV + 128] = ckv
        wA[l, :, O_KR + 64:O_KR + 96] = kr
        for p in range(8):
            jj, base = p // 3, 32 * (p % 3)
            wA[l, :, O_DQ + jj * 96 + base:O_DQ + jj * 96 + base + 32] = dq[:, p * 32:(p + 1) * 32]
            wA[l, :, O_DK + jj * 96 + base:O_DK + jj * 96 + base + 32] = dk[:, p * 32:(p + 1) * 32]
        wA[l, :, O_DV:O_DV + 256] = dv
        for jj in range(4):
            wA[l, :, O_NQ + jj * 128:O_NQ + jj * 128 + 64] = nq[:, jj * 64:(jj + 1) * 64]
            wA[l, :, O_NQ + jj * 128 + 64:O_NQ + (jj + 1) * 128] = nq[:, (jj + 4) * 64:(jj + 5) * 64]
        for br in range(3):
            wA[l, :, O_NK + br * 128:O_NK + (br + 1) * 128] = nkv[:, br * 256:br * 256 + 128]
        wA[l, :, O_NVC:O_NVC + 128] = nkv[:, 128:256]
        wA[l, :, O_NVS:O_NVS + 128] = nkv[:, 256 + 128:256 + 256]
        wA[l, :, O_NVS + 128:O_NVS + 256] = nkv[:, 512 + 128:512 + 256]
        wA[l, :, O_NG:O_NG + 24] = ng
        wk = np.asarray(inp["mla_w_ukv"][l], dtype=f)
        for h in range(4):
            wukvA[l, :, h * 96:h * 96 + 64] = wk[:, h * 128:h * 128 + 64]
            wukvA[l, :, 384 + h * 64:384 + (h + 1) * 64] = wk[:, h * 128 + 64:(h + 1) * 128]
    Wd["w_inA"] = wA

    def swp(a, blk):
        n = a.shape[-1] // blk
        b_ = a.reshape(a.shape[:-1] + (n, 2, blk // 2))
        return np.ascontiguousarray(b_[..., ::-1, :].reshape(a.shape))
    wS = np.zeros((L, D, NSW), f)
    kr_pad = wA[:, :, O_KR:O_KR + 96].copy()
    kr_pad[:, :, 64:96] = swp(kr_pad[:, :, 64:96], 32)
    wS[:, :, SW_KR:SW_KR + 96] = kr_pad
    wS[:, :, SW_DQ:SW_DQ + 288] = swp(wA[:, :, O_DQ:O_DQ + 288], 32)
    wS[:, :, SW_DK:SW_DK + 288] = swp(wA[:, :, O_DK:O_DK + 288], 32)
    wS[:, :, SW_NQ:SW_NQ + 512] = swp(wA[:, :, O_NQ:O_NQ + 512], 64)
    wS[:, :, SW_NK:SW_NK + 384] = swp(wA[:, :, O_NK:O_NK + 384], 64)
    Wd["w_inS"] = wS
    wq = np.asarray(inp["mla_w_uq"], dtype=f).reshape(L, 256, 4, 96)
    wqS = np.zeros_like(wq)
    wqS[..., 64:96] = swp(wq[..., 64:96], 32)
    Wd["w_uqS"] = np.ascontiguousarray(wqS.reshape(L, 256, 384))
    Wd["w_uq"] = np.ascontiguousarray(inp["mla_w_uq"], dtype=f)
    Wd["w_ukvA"] = wukvA
    Wd["qn_g"] = np.stack([pc(inp["mla_q_norm_g"][l]) for l in range(L)]).astype(f)
    Wd["kvn_g"] = np.stack([pc(inp["mla_kv_norm_g"][l]) for l in range(L)]).astype(f)
    rep = lambda v: np.ascontiguousarray(np.broadcast_to(np.asarray(v, dtype=f).reshape(1, -1), (128, np.asarray(v).size)))
    Wd["gate_b"] = np.stack([rep(inp["nsa_gate_b"][l]) for l in range(L)])
    Wd["dlam"] = np.stack([rep(inp["diff_lambda"][l]) for l in range(L)])
    Wd["dng"] = np.stack([rep(inp["diff_norm_g"][l]) for l in range(L)])
    cpos = np.zeros((L, 2, 128, 32), f); cw1 = np.zeros((L, 2, 128, 8192), f)
    cb1 = np.zeros((L, 2, 128, 2), f); cw2 = np.zeros((L, 2, 128, 256), f)
    for l in range(L):
        for kv in range(2):
            pT = np.asarray(inp["nsa_cmp_pos"][l, kv], dtype=f).T
            cpos[l, kv, 0:64] = pT; cpos[l, kv, 64:128] = pT
            w1 = np.asarray(inp["nsa_cmp_w1"][l, kv], dtype=f).reshape(32, 64, 256).transpose(1, 0, 2).reshape(64, 8192)
            cw1[l, kv, 0:64] = w1; cw1[l, kv, 64:128] = w1
            cb1[l, kv] = pc(np.asarray(inp["nsa_cmp_b1"][l, kv], dtype=f))
            w2 = np.asarray(inp["nsa_cmp_w2"][l, kv], dtype=f).reshape(2, 128, 64).transpose(1, 0, 2)
            tmp = np.zeros((128, 2, 128), f); tmp[:, :, 64:128] = w2
            cw2[l, kv] = tmp.reshape(128, 256)
    Wd.update(cpos=cpos, cw1=cw1, cb1=cb1, cw2=cw2)
    Wd["w_out"] = np.ascontiguousarray(inp["w_out"], dtype=f)
    return Wd


_CACHE = {}


def kernel(**inputs):
    x = np.asarray(inputs["x"], dtype=np.float32)
    B, T, _ = x.shape
    if T not in _CACHE:
        _CACHE[T] = build(T)[0]
    nc = _CACHE[T]
    Wd = host_weights(inputs)
    Wd.update(host_consts(T))
    in_maps = []
    for b in range(B):
        m = dict(Wd)
        m["xT"] = np.ascontiguousarray(x[b].T)
        in_maps.append(m)
    res = run_bass_kernel_spmd(nc, in_maps, core_ids=list(range(B)))
    out = np.stack([np.asarray(r["out"]).T for r in res.results]).astype(np.float32)
    return out
```
